# Optimizing a Trainium2 kernel written in Bass

```python
import jax, jax.numpy as jnp
from jax import lax
import numpy as np

D_MODEL = 4096
BATCH = 4
SEQ = 4096
DEPTH = 1

D_MIX = D_MODEL
LRU_WIDTH = D_MIX // 2
LRU_HEADS = 16
LRU_BLOCK = LRU_WIDTH // LRU_HEADS
CONV_WIDTH = 4
LRU_C = 8.0
HG_WIDTH = D_MIX - LRU_WIDTH
HG_HEAD_DIM = 128
HG_HEADS = HG_WIDTH // HG_HEAD_DIM
CHUNK = 64
IN_COLS = 2 * LRU_WIDTH + 4 * HG_WIDTH
SPLITS = [LRU_WIDTH, 2 * LRU_WIDTH, 2 * LRU_WIDTH + HG_WIDTH, 2 * LRU_WIDTH + 2 * HG_WIDTH, 2 * LRU_WIDTH + 3 * HG_WIDTH]
D_FF = 3 * D_MODEL
N_MEM = 256
X_HEADS = 4
X_HEAD_DIM = D_MODEL // X_HEADS
EPS = 1e-6

kernel_name = "hymba_style_rglru_hgrn2_macaron_memxattn"


def rmsnorm(x, w):
    xf = x.astype(jnp.float32)
    y = xf * lax.rsqrt(jnp.mean(xf * xf, axis=-1, keepdims=True) + EPS)
    return (y * w.astype(jnp.float32)).astype(x.dtype)


def swiglu(x, w_gate, w_up, w_down):
    return (jax.nn.silu(x @ w_gate) * (x @ w_up)) @ w_down


def causal_depthwise_conv(x, w, b):
    y = lax.conv_general_dilated(x, w, window_strides=(1,), padding=[(CONV_WIDTH - 1, 0)],
                                 dimension_numbers=('NWC', 'WIO', 'NWC'), feature_group_count=x.shape[-1])
    return y + b


def rg_lru_group(xb, gate, conv_w, conv_b, w_a, b_a, w_x, b_x, lam):
    B, S, W = xb.shape
    xc = causal_depthwise_conv(xb, conv_w, conv_b)
    xh = xc.reshape(B, S, LRU_HEADS, LRU_BLOCK)
    r = jax.nn.sigmoid(jnp.einsum('bshi,hij->bshj', xh, w_a) + b_a).reshape(B, S, W)
    i = jax.nn.sigmoid(jnp.einsum('bshi,hij->bshj', xh, w_x) + b_x).reshape(B, S, W)
    log_a = -LRU_C * r.astype(jnp.float32) * jax.nn.softplus(-lam.astype(jnp.float32))
    a = jnp.exp(log_a)
    bval = jnp.sqrt(-jnp.expm1(2.0 * log_a)) * (i.astype(jnp.float32) * xc.astype(jnp.float32))

    def combine(l, rgt):
        a1, b1 = l
        a2, b2 = rgt
        return a1 * a2, a2 * b1 + b2

    _, h = lax.associative_scan(combine, (a, bval), axis=1)
    return (h * jax.nn.gelu(gate.astype(jnp.float32))).astype(xb.dtype)


def hgrn2_group(q, f_pre, v, g, lb, gnorm_w):
    B, S, _ = q.shape
    N = S // CHUNK
    f32 = jnp.float32
    lb = lb.astype(f32)
    f = lb + (1.0 - lb) * jax.nn.sigmoid(f_pre.astype(f32))
    log_f = jnp.log(f)
    k = 1.0 - f
    q = jax.nn.silu(q.astype(f32))
    v = v.astype(f32)

    def to_chunks(t):
        return t.reshape(B, N, CHUNK, HG_HEADS, HG_HEAD_DIM).transpose(1, 0, 3, 2, 4)

    causal = jnp.tril(jnp.ones((CHUNK, CHUNK), dtype=bool))[:, :, None]

    def chunk_step(state, inp):
        qc, kc, vc, lc = inp
        bcum = jnp.cumsum(lc, axis=2)
        o_inter = jnp.einsum('bhtk,bhkv->bhtv', qc * jnp.exp(bcum), state)
        diff = bcum[:, :, :, None, :] - bcum[:, :, None, :, :]
        decay = jnp.exp(jnp.where(causal, diff, -jnp.inf))
        scores = jnp.einsum('bhtk,bhsk,bhtsk->bhts', qc, kc, decay)
        o = o_inter + jnp.einsum('bhts,bhsv->bhtv', scores, vc)
        b_last = bcum[:, :, -1:, :]
        new_state = jnp.exp(b_last[:, :, 0, :])[..., None] * state + \
            jnp.einsum('bhsk,bhsv->bhkv', kc * jnp.exp(b_last - bcum), vc)
        return new_state, o

    s0 = jnp.zeros((B, HG_HEADS, HG_HEAD_DIM, HG_HEAD_DIM), f32)
    _, o = lax.scan(chunk_step, s0, (to_chunks(q), to_chunks(k), to_chunks(v), to_chunks(log_f)))
    o = o.transpose(1, 0, 3, 2, 4).reshape(B, S, HG_HEADS, HG_HEAD_DIM)
    o = o * lax.rsqrt(jnp.mean(o * o, axis=-1, keepdims=True) + EPS) * gnorm_w.astype(f32)
    o = o * jax.nn.silu(g.astype(f32)).reshape(B, S, HG_HEADS, HG_HEAD_DIM)
    return o.reshape(B, S, HG_WIDTH).astype(g.dtype)


def mem_cross_attention(xn, memn, w_q, w_k, w_v, w_o):
    B, S, _ = xn.shape
    q = (xn @ w_q).reshape(B, S, X_HEADS, X_HEAD_DIM)
    k = (memn @ w_k).reshape(B, N_MEM, X_HEADS, X_HEAD_DIM)
    v = (memn @ w_v).reshape(B, N_MEM, X_HEADS, X_HEAD_DIM)
    s = jnp.einsum('bshd,bmhd->bhsm', q.astype(jnp.float32), k.astype(jnp.float32)) * (X_HEAD_DIM ** -0.5)
    p = jax.nn.softmax(s, axis=-1).astype(v.dtype)
    o = jnp.einsum('bhsm,bmhd->bshd', p, v).reshape(B, S, D_MODEL)
    return o @ w_o


def setup_inputs(seed: int = 0) -> dict:
    key = jax.random.key(seed)
    ks = iter(jax.random.split(key, 40))
    f32 = jnp.float32

    def nrm(shape, fan_in):
        return jax.random.normal(next(ks), shape, f32) * (fan_in ** -0.5)

    def gain(shape):
        return 1.0 + 0.01 * jax.random.normal(next(ks), shape, f32)

    def bias(shape):
        return 0.01 * jax.random.normal(next(ks), shape, f32)

    L = DEPTH
    u = jax.random.uniform(next(ks), (L, LRU_WIDTH), f32, 0.9, 0.999)
    a0 = u ** (1.0 / LRU_C)
    lru_lambda = jnp.log(a0) - jnp.log1p(-a0)
    return {
        "x": jax.random.normal(next(ks), (BATCH, SEQ, D_MODEL), f32),
        "mem": jax.random.normal(next(ks), (BATCH, N_MEM, D_MODEL), f32),
        "ffn1_norm": gain((L, D_MODEL)),
        "ffn1_w_gate": nrm((L, D_MODEL, D_FF), D_MODEL),
        "ffn1_w_up": nrm((L, D_MODEL, D_FF), D_MODEL),
        "ffn1_w_down": nrm((L, D_FF, D_MODEL), D_FF),
        "mix_norm": gain((L, D_MODEL)),
        "w_in": nrm((L, D_MODEL, IN_COLS), D_MODEL),
        "conv_w": nrm((L, CONV_WIDTH, 1, LRU_WIDTH), CONV_WIDTH),
        "conv_b": bias((L, LRU_WIDTH)),
        "lru_w_a": nrm((L, LRU_HEADS, LRU_BLOCK, LRU_BLOCK), LRU_BLOCK),
        "lru_b_a": bias((L, LRU_HEADS, LRU_BLOCK)),
        "lru_w_x": nrm((L, LRU_HEADS, LRU_BLOCK, LRU_BLOCK), LRU_BLOCK),
        "lru_b_x": bias((L, LRU_HEADS, LRU_BLOCK)),
        "lru_lambda": lru_lambda,
        "hg_lb_logits": 0.5 * jax.random.normal(next(ks), (L + 1, HG_WIDTH), f32),
        "hg_gnorm": gain((L, HG_HEAD_DIM)),
        "w_out": nrm((L, D_MIX, D_MODEL), D_MIX),
        "xattn_norm": gain((L, D_MODEL)),
        "mem_norm": gain((L, D_MODEL)),
        "xattn_w_q": nrm((L, D_MODEL, D_MODEL), D_MODEL),
        "xattn_w_k": nrm((L, D_MODEL, D_MODEL), D_MODEL),
        "xattn_w_v": nrm((L, D_MODEL, D_MODEL), D_MODEL),
        "xattn_w_o": nrm((L, D_MODEL, D_MODEL), D_MODEL),
        "ffn2_norm": gain((L, D_MODEL)),
        "ffn2_w_gate": nrm((L, D_MODEL, D_FF), D_MODEL),
        "ffn2_w_up": nrm((L, D_MODEL, D_FF), D_MODEL),
        "ffn2_w_down": nrm((L, D_FF, D_MODEL), D_FF),
        "final_norm": gain((D_MODEL,)),
    }


def reference(x, mem, ffn1_norm, ffn1_w_gate, ffn1_w_up, ffn1_w_down, mix_norm, w_in, conv_w, conv_b,
              lru_w_a, lru_b_a, lru_w_x, lru_b_x, lru_lambda, hg_lb_logits, hg_gnorm, w_out,
              xattn_norm, mem_norm, xattn_w_q, xattn_w_k, xattn_w_v, xattn_w_o,
              ffn2_norm, ffn2_w_gate, ffn2_w_up, ffn2_w_down, final_norm):
    lower_bounds = jnp.cumsum(jax.nn.softmax(hg_lb_logits.astype(jnp.float32), axis=0), axis=0)
    h = x
    for l in range(DEPTH):
        h = h + 0.5 * swiglu(rmsnorm(h, ffn1_norm[l]), ffn1_w_gate[l], ffn1_w_up[l], ffn1_w_down[l])
        u = rmsnorm(h, mix_norm[l]) @ w_in[l]
        lru_x, lru_g, hq, hf, hi, hg = jnp.split(u, SPLITS, axis=-1)
        y_a = rg_lru_group(lru_x, lru_g, conv_w[l], conv_b[l], lru_w_a[l], lru_b_a[l],
                           lru_w_x[l], lru_b_x[l], lru_lambda[l])
        y_b = hgrn2_group(hq, hf, hi, hg, lower_bounds[l], hg_gnorm[l])
        h = h + jnp.concatenate([y_a, y_b], axis=-1) @ w_out[l]
        h = h + mem_cross_attention(rmsnorm(h, xattn_norm[l]), rmsnorm(mem, mem_norm[l]),
                                    xattn_w_q[l], xattn_w_k[l], xattn_w_v[l], xattn_w_o[l])
        h = h + 0.5 * swiglu(rmsnorm(h, ffn2_norm[l]), ffn2_w_gate[l], ffn2_w_up[l], ffn2_w_down[l])
    return rmsnorm(h, final_norm)
```

```python
import os
import numpy as np
import concourse.bass as bass
import concourse.mybir as mybir
from concourse.bass_utils import run_bass_kernel_spmd

F32 = mybir.dt.float32
BF16 = mybir.dt.bfloat16
AF = mybir.ActivationFunctionType
ALU = mybir.AluOpType
EPS = 1e-6
SAME_ENGINE_SYNC = True
NOSYNC_ENGS = set(os.environ.get('NOSYNC', '').split(',')) - {''}
NW = 4
TT = 512


class Cfg:
    def __init__(self, D, DFF, S, NM=256, split=True):
        self.split = split
        self.D = D; self.KC = D // 128; self.DFF = DFF; self.NF = DFF // 128; self.NS = DFF // D
        self.S = S; self.T = S // 2 if split else S; self.NTO = self.T // TT; self.NTF = self.NTO
        self.NTP = self.NTO if split else 0
        self.LW = D // 2; self.HW = D // 2
        self.LCH = self.LW // 128; self.LC = self.LCH
        self.HH = self.HW // 128; self.HC = self.HH
        self.YC = self.LC + self.HC
        self.NM = NM; self.XH = 4; self.DH = D // 4; self.DHC = self.DH // 128
        self.WIN_HALF = 2 * self.LC + 4 * self.HC
        self.groups = [("f1gu", 2 * self.NF), ("f1d", self.NS * self.KC), ("win", self.WIN_HALF),
                       ("wout", self.KC), ("wq", self.KC), ("wk", self.KC), ("wv", self.KC), ("wo", self.KC),
                       ("f2gu", 2 * self.NF), ("f2d", self.NS * self.KC)]
        self.ntile8 = sum(c for _, c in self.groups)
        KC, LC, HC = self.KC, self.LC, self.HC
        o = 0
        self.o_n = {}
        for nm in ("n1", "nmix", "nx", "nmem", "n2", "nfin"):
            self.o_n[nm] = o; o += KC
        self.o_cw = o; o += LC * 4
        self.o_cb = o; o += LC
        self.o_ba = o; o += LC
        self.o_bx = o; o += LC
        self.o_lam = o; o += LC
        self.o_lb0 = o; o += HC
        self.o_lb1 = o; o += HC
        self.o_gn = o; o += 1
        self.o_flag = o; o += 1
        self.nsm = o
        self.o_id = 0; self.o_mk = 128; self.o_cm = 256; self.ncf = 256 + TT


class Op:
    __slots__ = ("idx", "eng", "fn", "deps", "dma", "signal", "val", "waits", "amt")

    def __init__(self, idx, eng, fn, deps, dma, amt):
        self.idx = idx; self.eng = eng; self.fn = fn; self.deps = deps; self.dma = dma
        self.signal = False; self.val = 0; self.waits = []; self.amt = amt


class Sched:
    ENGS = ("pe", "act", "dve", "pool", "sp")

    def __init__(self):
        self.ops = []
        self.lastw = {}
        self.readers = {}
        self.psum = set()
        self.rec = None

    def record(self, f):
        old = self.rec
        self.rec = []
        f()
        out = self.rec
        self.rec = old
        return out

    def play(self, *lists):
        lists = [l for l in lists if l]
        pos = [0] * len(lists)
        total = sum(len(l) for l in lists)
        for _ in range(total):
            k = min((i for i in range(len(lists)) if pos[i] < len(lists[i])), key=lambda i: pos[i] / len(lists[i]))
            self.op(*lists[k][pos[k]][:4], dma=lists[k][pos[k]][4], amt=lists[k][pos[k]][5])
            pos[k] += 1

    def op(self, eng, fn, r=(), w=(), dma=None, amt=16):
        if self.rec is not None:
            self.rec.append((eng, fn, tuple(r), tuple(w), dma, amt))
            return None
        deps = set()
        for b in r:
            x = self.lastw.get(b)
            if x is not None:
                deps.add(x)
            if b in self.psum:
                deps.update(y for y in self.readers.get(b, ()) if self.ops[y].eng != eng)
        for b in w:
            x = self.lastw.get(b)
            if x is not None:
                deps.add(x)
            deps.update(self.readers.get(b, ()))
        idx = len(self.ops)
        o = Op(idx, eng, fn, deps, dma, amt)
        self.ops.append(o)
        for b in r:
            self.readers.setdefault(b, []).append(idx)
        for b in w:
            self.lastw[b] = idx
            self.readers[b] = []
        return o

    def finalize(self, nc, sems):
        ops = self.ops
        for o in ops:
            best = {}
            for d in o.deps:
                dop = ops[d]
                if dop.dma is not None:
                    key = ("dma", dop.dma)
                else:
                    if dop.eng == o.eng and o.dma is None and (o.eng == "pe" or not SAME_ENGINE_SYNC or o.eng in NOSYNC_ENGS):
                        continue
                    key = ("eng", dop.eng)
                if key not in best or best[key] < d:
                    best[key] = d
            o.deps = best
            for d in best.values():
                ops[d].signal = True
        cnt = {}
        for o in ops:
            if o.dma is not None:
                k = ("dma", o.dma)
                cnt[k] = cnt.get(k, 0) + o.amt
                o.val = cnt[k]
                o.signal = True
            elif o.signal:
                k = ("eng", o.eng)
                cnt[k] = cnt.get(k, 0) + 1
                o.val = cnt[k]
        for o in ops:
            o.waits = [(k, ops[d].val) for k, d in o.deps.items()]
        self.maxcnt = cnt

    def emit(self, nc, block, sems):
        decos = {"pe": block.tensor, "act": block.scalar, "dve": block.vector, "pool": block.gpsimd, "sp": block.sync}
        for en in self.ENGS:
            eops = [o for o in self.ops if o.eng == en]

            def body(e, eops=eops):
                waited = {}
                ctx = {}
                for o in eops:
                    for k, v in o.waits:
                        if waited.get(k, 0) < v:
                            e.wait_ge(sems[k], v)
                            waited[k] = v
                    ins = o.fn(e, ctx)
                    if o.signal:
                        assert ins is not None
                        if o.dma is not None:
                            ins.then_inc(sems[("dma", o.dma)], o.amt)
                        else:
                            ins.then_inc(sems[("eng", o.eng)], 1)
            decos[en](body)


def build(cfg, dbg=None):
    c = cfg
    D, KC, NF, NS, T, NTO, NTF, LC, HC, YC, NM = c.D, c.KC, c.NF, c.NS, c.T, c.NTO, c.NTF, c.LC, c.HC, c.YC, c.NM
    nc = bass.Bass("TRN2", target_bir_lowering=False)
    xT = nc.dram_tensor("xT", [D, T], F32, kind="ExternalInput").ap()
    memT = nc.dram_tensor("memT", [D, NM], F32, kind="ExternalInput").ap()
    xP = nc.dram_tensor("xP", [D, T], F32, kind="ExternalInput").ap() if c.split else None
    wsh = nc.dram_tensor("wsh", [c.ntile8 * 128, D], F32, kind="ExternalInput").ap()
    smd = nc.dram_tensor("smd", [128, c.nsm], F32, kind="ExternalInput").ap()
    cfd = nc.dram_tensor("cfd", [128, c.ncf], F32, kind="ExternalInput").ap()
    gwd = nc.dram_tensor("gwd", [128, 2 * LC * 128], F32, kind="ExternalInput").ap()
    outT = nc.dram_tensor("outT", [D, T], F32, kind="ExternalOutput").ap()
    dbgo = None
    if dbg:
        dbgo = nc.dram_tensor("dbgo", [D, T], F32, kind="ExternalOutput").ap()
    wg = {}
    for gn, cnt in c.groups:
        wg[gn] = nc.dram_tensor("wg_" + gn, [cnt * 128, D], BF16)
    hs = nc.dram_tensor("hs", [D, T], F32)
    xn2i = nc.dram_tensor("xn2i", [D, T], BF16)
    yi = nc.dram_tensor("yi", [YC * 128, T], BF16)
    kvs = nc.dram_tensor("kvs", [128, 2 * KC * NM], BF16)

    S = Sched()
    S.psum = {"PB%d" % i for i in range(7)} | {"PT"}
    NXR = max(KC, 32)
    NHD = max(KC, 32)
    import contextlib
    es = contextlib.ExitStack()
    with es:
        def sb(name, shape, dt):
            return es.enter_context(nc.sbuf_tensor(name, shape, dt))

        def ps(name, shape, dt):
            return es.enter_context(nc.psum_tensor(name, shape, dt))
        XR = sb("XR", [128, NXR, TT], F32)
        XN = sb("XN", [128, KC, TT], BF16)
        HD = sb("HD", [128, NHD, TT], BF16)
        WS = sb("WS", [128, NW, D], BF16)
        TMP = sb("TMP", [128, 4, TT], F32)
        TMPB = sb("TMPB", [128, 4, TT], BF16)
        SM = sb("SM", [128, c.nsm], F32)
        CF = sb("CF", [128, c.ncf], F32)
        GWB = sb("GWB", [128, 2 * LC * 128], BF16)
        IDB = sb("IDB", [128, 128], BF16)
        MKB = sb("MKB", [128, 128], BF16)
        ONESB = sb("ONESB", [128, 128], BF16)
        RS = sb("RS", [128, TT], F32)
        NWS = sb("NWS", [128, 6 * KC], F32)
        SLAM = sb("SLAM", [128, LC], F32)
        LB = sb("LB", [128, 2 * HC], F32)
        GNW = sb("GNW", [128, 1], F32)
        LXW = sb("LXW", [128, 2, TT + 3], F32)
        HALO = sb("HALO", [128, LC, 3], F32)
        HCAR = sb("HCAR", [128, LC], F32)
        SST = sb("SST", [128, HC, 128], F32)
        SSB = sb("SSB", [128, HC, 128], BF16)
        EBL2 = sb("EBL", [128, 2, 8], F32)
        SMX2 = sb("SMX", [128, 16], F32)
        if HC * 128 >= 2 * TT + 2 * NM:
            SSBf = SSB[:, :, :].rearrange("p a b -> p (a b)")
            PNT = SSBf[:, 0:2 * TT].rearrange("p (a b) -> p a b", a=2)
            PRB = SSBf[:, 2 * TT:2 * TT + 2 * NM].rearrange("p (a b) -> p a b", a=2)
        else:
            PNT = sb("PNT", [128, 2, TT], BF16)
            PRB = sb("PRB", [128, 2, NM], BF16)
        PB = [ps("PB%d" % i, [128, TT], F32) for i in range(7)]
        PT = ps("PT", [128, 1024], BF16)

        cnt_names = {}

        def X(i):
            return XR[:, i, :]

        def N_(i):
            return XN[:, i, :]

        def H_(i):
            return HD[:, i, :]

        def dma(eng, out, in_, r, w, key):
            S.op(eng, lambda e, ctx, out=out, in_=in_: e.dma_start(out=out, in_=in_), r=r, w=w, dma=key)

        def act(out, in_, func, r, w, bias=None, scale=None, accum=None):
            kw = {}
            if bias is not None:
                kw["bias"] = bias
            if scale is not None:
                kw["scale"] = scale
            if accum is not None:
                kw["accum_out"] = accum
            S.op("act", lambda e, ctx: e.activation(out, in_, func, **kw), r=r, w=w)

        def tt(eng, out, a, b, op, r, w):
            S.op(eng, lambda e, ctx: e.tensor_tensor(out, a, b, op), r=r, w=w)

        def ts(eng, out, a, s1, s2, op0, op1, r, w):
            if op1 is None:
                S.op(eng, lambda e, ctx: e.tensor_scalar(out, a, s1, None, op0), r=r, w=w)
            else:
                S.op(eng, lambda e, ctx: e.tensor_scalar(out, a, s1, s2, op0, op1), r=r, w=w)

        def stt(eng, out, a, s, b, op0, op1, r, w):
            S.op(eng, lambda e, ctx: e.scalar_tensor_tensor(out, a, s, b, op0, op1), r=r, w=w)

        def cp(eng, out, in_, r, w):
            if eng == "act":
                S.op("act", lambda e, ctx: e.copy(out, in_), r=r, w=w)
            else:
                S.op(eng, lambda e, ctx: e.tensor_copy(out, in_), r=r, w=w)

        def mmg(out, pairs, r, w):
            n = len(pairs)

            def fn(e, ctx):
                ins = None
                for i, (l, rr) in enumerate(pairs):
                    ins = e.matmul(out, l, rr, start=(i == 0), stop=(i == n - 1))
                return ins
            S.op("pe", fn, r=r, w=w)

        def mm1(out, l, rr, start, stop, r, w):
            S.op("pe", lambda e, ctx: e.matmul(out, l, rr, start=start, stop=stop, skip_group_check=True), r=r, w=w)

        def tr(out, in_, r, w):
            S.op("pe", lambda e, ctx: e.transpose(out, in_, IDB[:, :]), r=r + ["IDB"], w=w)

        wstate = {"n": 0, "c": 0}

        class WStream:
            def __init__(self, tiles, ring=None):
                self.tiles = tiles; self.issued = 0; self.cur = 0; self.slots = []; self.ring = ring; self.k = 0

            def _issue(self, k):
                gn, ti, dyn = self.tiles[k]
                if self.ring is None:
                    slot = wstate["n"] % NW
                    wstate["n"] += 1
                else:
                    slot = self.ring[self.k % len(self.ring)]
                    self.k += 1
                self.slots.append(slot)
                dst = WS[:, slot, :]
                src = wg[gn][ti * 128:(ti + 1) * 128, :]
                dma("sp", dst, src, ["wg_%s_%d" % (gn, ti // 16)], ["WS%d" % slot], "WS%d" % slot)

            def next(self):
                depth = NW if self.ring is None else len(self.ring) + 1
                while self.issued < len(self.tiles) and self.issued < self.cur + depth - 1:
                    self._issue(self.issued); self.issued += 1
                if self.issued <= self.cur:
                    self._issue(self.issued); self.issued += 1
                slot = self.slots[self.cur]
                self.cur += 1
                return slot

        def wl(slot, kc):
            return WS[:, slot, kc * 128:(kc + 1) * 128]

        pbrot = {"i": 0}

        dma("sp", SM[:, :], smd[:, :], [], ["SM"], "SM")
        dma("sp", CF[:, :], cfd[:, :], [], ["CF"], "CF")
        dma("pool", GWB[:, :], gwd[:, :], [], ["GWB"], "GWB")
        cp("dve", IDB[:, :], CF[:, c.o_id:c.o_id + 128], ["CF"], ["IDB"])
        cp("dve", MKB[:, :], CF[:, c.o_mk:c.o_mk + 128], ["CF"], ["MKB"])
        S.op("dve", lambda e, ctx: e.memset(ONESB[:, :], 1.0), w=["ONESB"])
        S.op("dve", lambda e, ctx: e.memset(HALO[:, :, :], 0.0), w=["HALO%d" % i for i in range(LC)])
        S.op("dve", lambda e, ctx: e.memset(HCAR[:, :], 0.0), w=["HCAR%d" % i for i in range(LC)])
        S.op("dve", lambda e, ctx: e.memset(SST[:, :, :], 0.0), w=["SST%d" % i for i in range(HC)])
        S.op("dve", lambda e, ctx: e.memset(SSB[:, :, :], 0.0), w=["SSB%d" % i for i in range(HC)])
        ts("dve", NWS[:, :], SM[:, 0:6 * KC], float(np.sqrt(D)), None, ALU.mult, None, ["SM"], ["NWS"])
        act(SLAM[:, :], SM[:, c.o_lam:c.o_lam + LC], AF.Exp, ["SM"], ["SLAM"], scale=-1.0)
        act(SLAM[:, :], SLAM[:, :], AF.Ln, ["SLAM"], ["SLAM"], bias=1.0)
        ts("dve", SLAM[:, :], SLAM[:, :], -8.0, None, ALU.mult, None, ["SLAM"], ["SLAM"])
        tt("dve", LB[:, 0:HC], SM[:, c.o_lb0:c.o_lb0 + HC], SM[:, c.o_lb1:c.o_lb1 + HC], ALU.subtract, ["SM"], ["LB"])
        act(LB[:, 0:HC], LB[:, 0:HC], AF.Sigmoid, ["LB"], ["LB"])
        ts("dve", LB[:, HC:2 * HC], LB[:, 0:HC], -1.0, 1.0, ALU.mult, ALU.add, ["LB"], ["LB"])
        ts("dve", GNW[:, :], SM[:, c.o_gn:c.o_gn + 1], float(np.sqrt(128.0)), None, ALU.mult, None, ["SM"], ["GNW"])

        CH = 16
        grow = {}
        row = 0
        for gn, cnt in c.groups:
            grow[gn] = row; row += cnt
        cast_plan = []
        gcnt = dict(c.groups)
        f1 = []
        per_slab_gu = 2 * KC
        for s_ in range(NS):
            for k0 in range(s_ * per_slab_gu, (s_ + 1) * per_slab_gu, CH):
                f1.append(("f1gu", k0, min(k0 + CH, (s_ + 1) * per_slab_gu)))
            for k0 in range(s_ * KC, (s_ + 1) * KC, CH):
                f1.append(("f1d", k0, min(k0 + CH, (s_ + 1) * KC)))
        rest = []
        for gn, cnt in c.groups:
            if gn in ("f1gu", "f1d"):
                continue
            for k0 in range(0, cnt, CH):
                rest.append((gn, k0, min(cnt, k0 + CH)))
        cast_state = {"rest": rest}

        def cast_emit(chunks, extra_r=()):
            for gn, k0, k1 in chunks:
                src = wsh[(grow[gn] + k0) * 128:(grow[gn] + k1) * 128, :]
                kk = wstate["c"] % 4
                wstate["c"] += 1
                dma("pool", wg[gn][k0 * 128:k1 * 128, :], src, list(extra_r), ["wg_%s_%d" % (gn, k0 // CH), "wcastbuf%d" % kk],
                    "wcast%d" % kk)

        def cast_some(n, extra_r=()):
            r_ = cast_state["rest"]
            cast_emit(r_[:n], extra_r)
            cast_state["rest"] = r_[n:]
        first_src = (xP if c.NTP else xT).rearrange("(kc p) t -> p kc t", p=128)[:, :, 0:TT]
        preload = dbg is None
        if preload:
            dma("sp", XR[:, 0:KC, :], first_src, [], ["X%d" % i for i in range(KC)], "XR")
        if dbg != "w0":
            cast_emit(f1[0:1], ["X0"] if preload else [])
            cast_emit(f1[1:])
            if c.NTP:
                cast_emit([x_ for x_ in cast_state["rest"] if x_[0] == "win"])
                cast_state["rest"] = [x_ for x_ in cast_state["rest"] if x_[0] != "win"]

        sqs = {"i": 0}

        def sumsq_chunk(i, nchunks, width, src, srcn):
            SUM = PB[6]
            k = sqs["i"]; sqs["i"] += 1
            t_ = TMP[:, 2 + k % 2, 0:width]; tn = "TMP%d" % (2 + k % 2)
            act(t_, src(i), AF.Square, [srcn(i)], [tn])
            hi = TMPB[:, (2 * k) % 4, 0:width]; hin = "TMPB%d" % ((2 * k) % 4)
            lo = TMPB[:, (2 * k + 1) % 4, 0:width]; lon = "TMPB%d" % ((2 * k + 1) % 4)
            cp("dve", hi, t_, [tn], [hin])
            tt("dve", lo, t_, hi, ALU.subtract, [tn, hin], [lon])
            mm1(SUM[:, 0:width], ONESB[:, :], hi, i == 0, False, ["ONESB", hin], ["PB6"])
            mm1(SUM[:, 0:width], ONESB[:, :], lo, False, i == nchunks - 1, ["ONESB", lon], ["PB6"])

        def sumsq(nchunks, width, src, srcn):
            for i in range(nchunks):
                sumsq_chunk(i, nchunks, width, src, srcn)

        sq_pend = []
        SQ_LAG = 2

        def sumsq_push(i, nchunks, width, src, srcn):
            sq_pend.append((i, nchunks, width, src, srcn))
            if len(sq_pend) > SQ_LAG:
                sumsq_chunk(*sq_pend.pop(0))

        def sumsq_flush():
            while sq_pend:
                sumsq_chunk(*sq_pend.pop(0))

        def rmsnorm(nchunks, width, src, srcn, wcol, dst, dstn, dmodel, presummed=False):
            SUM = PB[6]
            if not presummed:
                sumsq(nchunks, width, src, srcn)
            act(RS[:, 0:width], SUM[:, 0:width], AF.Sqrt, ["PB6"], ["RS"], bias=float(dmodel * EPS), scale=1.0)
            S.op("dve", lambda e, ctx: e.reciprocal(RS[:, 0:width], RS[:, 0:width]), r=["RS"], w=["RS"])
            for i in range(nchunks):
                stt("dve", dst(i), src(i), NWS[:, wcol + i:wcol + i + 1], RS[:, 0:width], ALU.mult, ALU.mult,
                    [srcn(i), "NWS", "RS"], [dstn(i)])

        Xn = lambda i: "X%d" % i
        Nn = lambda i: "N%d" % i
        Hn = lambda i: "H%d" % i

        def ffn(gu, dn, wcol, presummed=False):
            rmsnorm(KC, TT, X, Xn, wcol, N_, Nn, D, presummed)
            tiles = []
            for s_ in range(NS):
                for j in range(KC):
                    jj = s_ * KC + j
                    tiles.append((gu, 2 * jj, 0)); tiles.append((gu, 2 * jj + 1, 0))
                for i in range(KC):
                    tiles.append((dn, s_ * KC + i, 0))
            wsq = WStream(tiles)
            for s_ in range(NS):
                for j in range(KC):
                    sg = wsq.next(); su = wsq.next()
                    G = PB[(2 * j) % 4]; U = PB[(2 * j + 1) % 4]
                    Gn = "PB%d" % ((2 * j) % 4); Un = "PB%d" % ((2 * j + 1) % 4)
                    mmg(G[:, :], [(wl(sg, k), N_(k)) for k in range(KC)], ["WS%d" % sg] + [Nn(k) for k in range(KC)], [Gn])
                    mmg(U[:, :], [(wl(su, k), N_(k)) for k in range(KC)], ["WS%d" % su] + [Nn(k) for k in range(KC)], [Un])
                    t_ = TMP[:, j % 2, :]
                    act(t_, G[:, :], AF.Silu, [Gn], ["TMP%d" % (j % 2)])
                    tt("dve", H_(j), t_, U[:, :], ALU.mult, ["TMP%d" % (j % 2), Un], [Hn(j)])
                for i in range(KC):
                    sd = wsq.next()
                    Dp = PB[4 + (i % 2)]; Dn = "PB%d" % (4 + (i % 2))
                    mmg(Dp[:, :], [(wl(sd, k), H_(k)) for k in range(KC)], ["WS%d" % sd] + [Hn(k) for k in range(KC)], [Dn])
                    stt("dve", X(i), Dp[:, :], 0.5, X(i), ALU.mult, ALU.add, [Dn, Xn(i)], [Xn(i)])
                    if s_ == NS - 1:
                        sumsq_push(i, KC, TT, X, Xn)
            sumsq_flush()

        xT_v = xT.rearrange("(kc p) t -> p kc t", p=128)
        hs_v = hs.ap().rearrange("(kc p) t -> p kc t", p=128)
        out_v = outT.rearrange("(kc p) t -> p kc t", p=128)
        xn2i_v = xn2i.ap().rearrange("(kc p) t -> p kc t", p=128)
        yi_v = yi.ap().rearrange("(q p) t -> p q t", p=128)
        dbg_v = dbgo.rearrange("(kc p) t -> p kc t", p=128) if dbg else None

        if True:
            do_l = dbg != "p2h"; do_h = dbg != "p2l"
            CM = CF[:, c.o_cm:c.o_cm + TT]
            wtp = {"i": 0}

            def WT():
                i = wtp["i"] % NXR; wtp["i"] += 1
                return XR[:, i, :], "X%d" % i
            wbp = {"i": 0}

            def WB():
                i = wbp["i"] % NHD; wbp["i"] += 1
                return HD[:, i, :], "H%d" % i

            def ystore(yt, ytn, q, tf):
                dma("sp", yi_v[:, q, tf * TT:(tf + 1) * TT], yt, [ytn], ["yi%d" % tf, "ystchain"], "yst")
            prj = {"i": 0}

            def PJ(p=0, kind="head"):
                if kind == "lru":
                    k = prj.get(p, 0); prj[p] = k + 1
                    banks = ([2, 3], [4, 6])[p] if prj.get("so") else ([0, 1], [2, 3])[p]
                    i = banks[k % 2]
                else:
                    i = p
                return PB[i], "PB%d" % i

            def mixer_tile(tf, state_only):
                if not state_only:
                    dma("sp", XN[:, :, :], xn2i_v[:, :, tf * TT:(tf + 1) * TT], ["xn2i%d" % tf], [Nn(i) for i in range(KC)], "XN")

                def want(j):
                    if j < 2 * LC:
                        return do_l and (not state_only or j % 2 == 0)
                    return do_h and (not state_only or (j - 2 * LC) % 4 in (1, 2))
                def stream_of(j):
                    return (j // 2) % 2 if j < 2 * LC else ((j - 2 * LC) // 4) % 2
                prj["so"] = state_only
                if state_only:
                    wsqs = {(kind, p_): WStream([("win", j, 0) for j in range(c.WIN_HALF) if want(j) and stream_of(j) == p_
                                                 and ((j < 2 * LC) == (kind == "lru"))], ring=[(0 if kind == "lru" else 2) + p_])
                            for kind in ("lru", "head") for p_ in range(2)}
                else:
                    w2 = [WStream([("win", j, 0) for j in range(c.WIN_HALF) if want(j) and stream_of(j) == p_],
                                  ring=[2 * p_, 2 * p_ + 1]) for p_ in range(2)]
                    wsqs = {(kind, p_): w2[p_] for kind in ("lru", "head") for p_ in range(2)}
                Nall = [Nn(k) for k in range(KC)]

                def proj(p=0, kind="head"):
                    s_ = wsqs[(kind, p)].next()
                    P, Pn = PJ(p, kind)
                    mmg(P[:, :], [(wl(s_, k), N_(k)) for k in range(KC)], ["WS%d" % s_] + Nall, [Pn])
                    return P, Pn
                def lru_chunk(ch):
                    sp_ = ch % 2
                    P, Pn = proj(sp_, "lru")
                    Ln = "LXW%d" % (ch % 2)
                    LX = LXW[:, ch % 2, :]
                    cp("dve", LX[:, 0:3], HALO[:, ch, :], ["HALO%d" % ch], [Ln])
                    cp("act", LX[:, 3:TT + 3], P[:, :], [Pn], [Ln])
                    cp("dve", HALO[:, ch, :], LX[:, TT:TT + 3], [Ln], ["HALO%d" % ch])
                    xc, xcn = WT()
                    cw = lambda j, ch=ch: SM[:, c.o_cw + ch * 4 + j:c.o_cw + ch * 4 + j + 1]
                    ts("dve", xc, LX[:, 0:TT], cw(0), SM[:, c.o_cb + ch:c.o_cb + ch + 1], ALU.mult, ALU.add, [Ln, "SM"], [xcn])
                    for j in (1, 2, 3):
                        stt("dve", xc, LX[:, j:j + TT], cw(j), xc, ALU.mult, ALU.add, [Ln, "SM", xcn], [xcn])
                    xcb, xcbn = WB()
                    cp("pool", xcb, xc, [xcn], [xcbn])
                    Pr, Prn = PJ(sp_, "lru")
                    mmg(Pr[:, :], [(GWB[:, (0 * LC + ch) * 128:(0 * LC + ch + 1) * 128], xcb)], ["GWB", xcbn], [Prn])
                    Pi, Pin = PJ(sp_, "lru")
                    mmg(Pi[:, :], [(GWB[:, (1 * LC + ch) * 128:(1 * LC + ch + 1) * 128], xcb)], ["GWB", xcbn], [Pin])
                    rr, rrn = WT()
                    act(rr, Pr[:, :], AF.Sigmoid, [Prn, "SM"], [rrn], bias=SM[:, c.o_ba + ch:c.o_ba + ch + 1])
                    ig, ign = WT()
                    act(ig, Pi[:, :], AF.Sigmoid, [Pin, "SM"], [ign], bias=SM[:, c.o_bx + ch:c.o_bx + ch + 1])
                    act(rr, rr, AF.Exp, [rrn, "SLAM"], [rrn], scale=SLAM[:, ch:ch + 1])
                    a2, a2n = WT()
                    tt("pool", a2, rr, rr, ALU.mult, [rrn], [a2n])
                    act(a2, a2, AF.Sqrt, [a2n], [a2n], bias=1.0, scale=-1.0)
                    tt("pool", ig, ig, xc, ALU.mult, [ign, xcn], [ign])
                    tt("dve", ig, ig, a2, ALU.mult, [ign, a2n], [ign])
                    hh, hhn = WT()
                    S.op("dve", lambda e, ctx, hh=hh, rr=rr, ig=ig, ch=ch: e.tensor_tensor_scan(
                        hh, rr, ig, HCAR[:, ch:ch + 1], ALU.mult, ALU.add),
                        r=[rrn, ign, "HCAR%d" % ch], w=[hhn])
                    cp("dve", HCAR[:, ch:ch + 1], hh[:, TT - 1:TT], [hhn], ["HCAR%d" % ch])
                    if state_only:
                        return
                    P, Pn = proj(sp_, "lru")
                    gs, gsn = WT()
                    cp("act", gs, P[:, :], [Pn], [gsn])
                    g2, g2n = WT()
                    tt("pool", g2, gs, gs, ALU.mult, [gsn], [g2n])
                    ts("pool", g2, g2, 0.044715, 1.0, ALU.mult, ALU.add, [g2n], [g2n])
                    tt("pool", g2, g2, gs, ALU.mult, [g2n, gsn], [g2n])
                    act(g2, g2, AF.Sigmoid, [g2n], [g2n], scale=1.5957691216057308)
                    tt("pool", g2, g2, gs, ALU.mult, [g2n, gsn], [g2n])
                    yt, ytn = WB()
                    tt("dve", yt, hh, g2, ALU.mult, [hhn, g2n], [ytn])
                    ystore(yt, ytn, ch, tf)
                for ch in range(0, LC if (do_l and not state_only) else 0, 2):
                    la = S.record(lambda: lru_chunk(ch))
                    lb_ = S.record(lambda: lru_chunk(ch + 1)) if ch + 1 < LC else []
                    S.play(la, lb_)
                def head(hd):
                    sp_ = hd % 2
                    EBL = EBL2[:, hd % 2, :]; EBn = "EBL%d" % (hd % 2)
                    if not state_only:
                        Pq, Pqn = proj(sp_)
                        q, qn = WT()
                        act(q, Pq[:, :], AF.Silu, [Pqn], [qn])
                    Pf, Pfn = proj(sp_)
                    f, fn_ = WT()
                    act(f, Pf[:, :], AF.Sigmoid, [Pfn], [fn_])
                    ts("dve", f, f, LB[:, HC + hd:HC + hd + 1], LB[:, hd:hd + 1], ALU.mult, ALU.add, [fn_, "LB"], [fn_])
                    lf, lfn = WT()
                    act(lf, f, AF.Ln, [fn_], [lfn])
                    ts("pool", f, f, -1.0, 1.0, ALU.mult, ALU.add, [fn_], [fn_])
                    bc, bcn = WT()
                    S.op("dve", lambda e, ctx, bc=bc, lf=lf: e.tensor_tensor_scan(bc, CM, lf, 0.0, ALU.mult, ALU.add),
                         r=["CF", lfn], w=[bcn])
                    S.op("act", lambda e, ctx, bc=bc: e.activation(
                        EBL[:, 0:TT // 64], bc.rearrange("p (c s) -> p c s", s=64)[:, :, 63], AF.Exp),
                        r=[bcn], w=[EBn])
                    if not state_only:
                        act(lf, bc, AF.Exp, [bcn], [lfn])
                        qd, qdn = WB()
                        tt("dve", qd, q, lf, ALU.mult, [qn, lfn], [qdn])
                    act(bc, bc, AF.Exp, [bcn], [bcn], scale=-1.0)
                    if not state_only:
                        kd, kdn = WB()
                        tt("pool", kd, f, bc, ALU.mult, [fn_, bcn], [kdn])
                    kd2, kd2n = WB()
                    for cc in range(TT // 64):
                        sl = slice(cc * 64, cc * 64 + 64)
                        stt("dve", kd2[:, sl], f[:, sl], EBL[:, cc:cc + 1], bc[:, sl], ALU.mult, ALU.mult,
                            [fn_, EBn, bcn], [kd2n])
                    Pv, Pvn = proj(sp_)
                    vb, vbn = WB()
                    cp("act", vb, Pv[:, :], [Pvn], [vbn])
                    if not state_only:
                        Pg, Pgn = proj(sp_)
                        gg, ggn = WT()
                        act(gg, Pg[:, :], AF.Silu, [Pgn], [ggn])
                    HCUT = int(os.environ.get("HCUT", "9"))
                    if HCUT < 1:
                        return
                    kT, kTn = WB()
                    vT, vTn = WB()
                    pt0 = sp_ * 512
                    for blk in range(TT // 128):
                        bs = slice(blk * 128, blk * 128 + 128)
                        tr(PT[:, pt0 + blk * 128:pt0 + (blk + 1) * 128], kd2[:, bs], [kd2n], ["PT"])
                    cp("dve", kT, PT[:, pt0:pt0 + 512], ["PT"], [kTn])
                    for blk in range(TT // 128):
                        bs = slice(blk * 128, blk * 128 + 128)
                        tr(PT[:, pt0 + blk * 128:pt0 + (blk + 1) * 128], vb[:, bs], [vbn], ["PT"])
                    cp("act", vT, PT[:, pt0:pt0 + 512], ["PT"], [vTn])
                    PO = PB[2 + hd % 2]; POn = "PB%d" % (2 + hd % 2); SC = PB[4]; DS = PB[5]
                    if HCUT < 2:
                        return
                    for blk in range(TT // 128):
                        bs = slice(blk * 128, blk * 128 + 128)
                        if not state_only:
                            scs = slice(sp_ * 256 + (blk % 2) * 128, sp_ * 256 + (blk % 2) * 128 + 128)
                            mm1(SC[:, scs], kd[:, bs], qd[:, bs], True, True, [kdn, qdn], ["PB4"])
                            scm, scmn = WB()
                            tt("dve", scm[:, 0:128], SC[:, scs], MKB[:, :], ALU.mult, ["PB4", "MKB"], [scmn])
                            mm1(PO[:, bs], vT[:, bs], scm[:, 0:128], True, False, [vTn, scmn], [POn])
                        for hf in range(2 if HCUT >= 3 else 0):
                            cc = blk * 2 + hf
                            cs = slice(cc * 64, cc * 64 + 64)
                            pr = slice(hf * 64, hf * 64 + 64)
                            if not state_only:
                                mm1(PO[:, cs], SSB[:, hd, :], qd[:, cs], False, hf == 1, ["SSB%d" % hd, qdn], [POn])
                            if HCUT < 4:
                                continue
                            dsl = slice(sp_ * 256 + (cc % 2) * 128, sp_ * 256 + (cc % 2) * 128 + 128)
                            mm1(DS[:, dsl], kT[pr, bs], vT[pr, bs], True, True, [kTn, vTn], ["PB5"])
                            stt("dve", SST[:, hd, :], SST[:, hd, :], EBL[:, cc:cc + 1], DS[:, dsl], ALU.mult, ALU.add,
                                ["SST%d" % hd, EBn, "PB5"], ["SST%d" % hd])
                            if not state_only:
                                cp("act", SSB[:, hd, :], SST[:, hd, :], ["SST%d" % hd], ["SSB%d" % hd])
                    if HCUT < 5 or state_only:
                        return
                    osq, osqn = WT()
                    act(osq, PO[:, :], AF.Square, [POn], [osqn])
                    ohi, ohin = WB()
                    cp("pool", ohi, osq, [osqn], [ohin])
                    olo, olon = WB()
                    tt("pool", olo, osq, ohi, ALU.subtract, [osqn, ohin], [olon])
                    mm1(PB[sp_][:, :], ONESB[:, :], ohi, True, False, ["ONESB", ohin], ["PB%d" % sp_])
                    mm1(PB[sp_][:, :], ONESB[:, :], olo, False, True, ["ONESB", olon], ["PB%d" % sp_])
                    act(osq, PB[sp_][:, :], AF.Sqrt, ["PB%d" % sp_], [osqn], bias=float(128 * EPS), scale=1.0)
                    S.op("dve", lambda e, ctx, osq=osq: e.reciprocal(osq, osq), r=[osqn], w=[osqn])
                    tt("dve", osq, PO[:, :], osq, ALU.mult, [POn, osqn], [osqn])
                    yt, ytn = WB()
                    stt("dve", yt, osq, GNW[:, 0:1], gg, ALU.mult, ALU.mult, [osqn, "GNW", ggn], [ytn])
                    ystore(yt, ytn, LC + hd, tf)
                for hd in range(0, HC if (do_h and not state_only) else 0, 2):
                    ha = S.record(lambda: head(hd))
                    hb = S.record(lambda: head(hd + 1)) if hd + 1 < HC else []
                    S.play(ha, hb)
                if state_only:
                    for i_ in range(0, max(LC, HC), 2):
                        ls = []
                        if do_l and i_ < LC:
                            ls.append(S.record(lambda: lru_chunk(i_)))
                            if i_ + 1 < LC:
                                ls.append(S.record(lambda: lru_chunk(i_ + 1)))
                        if do_h and i_ < HC:
                            ls.append(S.record(lambda: head(i_)))
                            if i_ + 1 < HC:
                                ls.append(S.record(lambda: head(i_ + 1)))
                        S.play(*ls)

        if dbg not in ("w0", "w1", "p1"):
            for t in range(c.NTP):
                t0 = t * TT
                if not (preload and t == 0):
                    dma("sp", XR[:, 0:KC, :], xP.rearrange("(kc p) t -> p kc t", p=128)[:, :, t0:t0 + TT], [], [Xn(i) for i in range(KC)], "XR")
                ffn("f1gu", "f1d", c.o_n["n1"])
                rmsnorm(KC, TT, X, Xn, c.o_n["nmix"], N_, Nn, D, presummed=True)
                mixer_tile(-1 - t, True)
                npace = max(1, c.NTP - 1)
                nrest = len(cast_state["rest"])
                if nrest:
                    cast_some(nrest if t >= npace - 1 else -(-nrest // (npace - t)), ["SST%d" % (HC - 1)])
            if c.NTP:
                fl = SM[:, c.o_flag:c.o_flag + 1]
                ts("dve", HCAR[:, :], HCAR[:, :], fl, None, ALU.mult, None, ["SM"] + ["HCAR%d" % i for i in range(LC)], ["HCAR%d" % i for i in range(LC)])
                HALOf = HALO[:, :, :].rearrange("p a b -> p (a b)")
                ts("dve", HALOf, HALOf, fl, None, ALU.mult, None, ["SM"] + ["HALO%d" % i for i in range(LC)], ["HALO%d" % i for i in range(LC)])
                SSTf = SST[:, :, :].rearrange("p a b -> p (a b)")
                ts("dve", SSTf, SSTf, fl, None, ALU.mult, None, ["SM"] + ["SST%d" % i for i in range(HC)], ["SST%d" % i for i in range(HC)])
                cp("act", SSB[:, :, :].rearrange("p a b -> p (a b)"), SSTf, ["SST%d" % i for i in range(HC)], ["SSB%d" % i for i in range(HC)])

        if dbg in ("w0", "w1"):
            dma("sp", XR[:, 0:KC, :], xT_v[:, :, 0:TT], [], [Xn(i) for i in range(KC)], "XR")
            rd = [Xn(i) for i in range(KC)] + ["SLAM", "LB", "GNW", "NWS", "IDB", "MKB", "GWB", "ONESB"]
            if dbg == "w1":
                rd += ["wg_%s_%d" % (gn, k // 16) for gn, cnt in c.groups for k in range(0, cnt, 16)]
            dma("sp", dbg_v[:, :, 0:TT], XR[:, 0:KC, :], rd, ["dbg"], "dbgst")
        for t in range(NTO if dbg not in ("w0", "w1") else 0):
            t0 = t * TT
            if not (preload and t == 0 and not c.NTP):
                dma("sp", XR[:, 0:KC, :], xT_v[:, :, t0:t0 + TT], [], [Xn(i) for i in range(KC)], "XR")
            ffn("f1gu", "f1d", c.o_n["n1"])
            dma("sp", hs_v[:, :, t0:t0 + TT], XR[:, 0:KC, :], [Xn(i) for i in range(KC)], ["hs%d" % t], "hsst")
            npace = max(1, NTO - 2)
            nrest = len(cast_state["rest"])
            if nrest:
                cast_some(nrest if t >= npace - 1 else -(-nrest // (npace - t)), ["hs%d" % t])
            rmsnorm(KC, TT, X, Xn, c.o_n["nmix"], N_, Nn, D, presummed=True)
            dma("sp", xn2i_v[:, :, t0:t0 + TT], XN[:, :, :], [Nn(i) for i in range(KC)], ["xn2i%d" % t], "xn2st")
            if dbg == "p1":
                dma("sp", dbg_v[:, :, t0:t0 + TT], XR[:, 0:KC, :], [Xn(i) for i in range(KC)], ["dbg"], "dbgst")


        if dbg not in ("p1", "w0", "w1"):
            for tf in range(NTF):
                mixer_tile(tf, False)
            if dbg in ("p2", "p2l", "p2h"):
                for t in range(NTO):
                    dma("sp", HD[:, 0:YC, :], yi_v[:, :, t * TT:(t + 1) * TT], ["yi%d" % t], [Hn(i) for i in range(YC)], "HDl")
                    for i in range(KC):
                        cp("dve", X(i), H_(i), [Hn(i)], [Xn(i)])
                    dma("sp", dbg_v[:, :, t * TT:(t + 1) * TT], XR[:, 0:KC, :], [Xn(i) for i in range(KC)], ["dbg"], "dbgst")

        if dbg is None or dbg == "p3":
            kvs_v = kvs.ap()
            dma("sp", XR[:, 0:KC, 0:NM], memT.rearrange("(kc p) m -> p kc m", p=128), [], [Xn(i) for i in range(KC)], "XR")
            rmsnorm(KC, NM, lambda i: XR[:, i, 0:NM], Xn, c.o_n["nmem"], lambda i: XN[:, i, 0:NM], Nn, D)
            wsq = WStream([("wk", j, 0) for j in range(KC)] + [("wv", j, 0) for j in range(KC)])
            for which in range(2):
                for j in range(KC):
                    s_ = wsq.next()
                    P = PB[j % 2]; Pn = "PB%d" % (j % 2)
                    mmg(P[:, 0:NM], [(wl(s_, k), XN[:, k, 0:NM]) for k in range(KC)], ["WS%d" % s_] + [Nn(k) for k in range(KC)], [Pn])
                    cp("act", HD[:, j, which * NM:(which + 1) * NM], P[:, 0:NM], [Pn], [Hn(j)])
            XNf = XN[:, :, :].rearrange("p a b -> p (a b)")
            for j in range(KC):
                for mb in range(NM // 128):
                    tr(PT[:, (j % 4) * 256 + mb * 128:(j % 4) * 256 + (mb + 1) * 128],
                       HD[:, j, NM + mb * 128:NM + (mb + 1) * 128], [Hn(j)], ["PT"])
                for mb in range(NM // 128):
                    cp("dve", XNf[:, mb * D + j * 128:mb * D + (j + 1) * 128],
                       PT[:, (j % 4) * 256 + mb * 128:(j % 4) * 256 + (mb + 1) * 128], ["PT"], [Nn(k) for k in range(KC)])
            dma("sp", kvs_v[:, 0:KC * NM].rearrange("p (j m) -> p j m", m=NM), HD[:, 0:KC, 0:NM], [Hn(j) for j in range(KC)], ["kvsK"], "kvst")
            dma("sp", kvs_v[:, KC * NM:2 * KC * NM], XNf[:, 0:2 * D], [Nn(k) for k in range(KC)], ["kvsV"], "kvst2")

            for t in range(NTO):
                t0 = t * TT
                dma("sp", XR[:, 0:KC, :], hs_v[:, :, t0:t0 + TT], ["hs%d" % t], [Xn(i) for i in range(KC)], "XR")

                dma("sp", HD[:, 0:YC, :], yi_v[:, :, t0:t0 + TT], ["yi%d" % t], [Hn(i) for i in range(YC)], "HDl")
                wsq = WStream([("wout", j, 0) for j in range(KC)] + [("wq", j, 0) for j in range(KC)])
                for i in range(KC):
                    s_ = wsq.next()
                    P = PB[i % 2]; Pn = "PB%d" % (i % 2)
                    mmg(P[:, :], [(wl(s_, k), H_(k)) for k in range(KC)], ["WS%d" % s_] + [Hn(k) for k in range(KC)], [Pn])
                    tt("dve", X(i), P[:, :], X(i), ALU.add, [Pn, Xn(i)], [Xn(i)])
                    sumsq_push(i, KC, TT, X, Xn)
                sumsq_flush()
                rmsnorm(KC, TT, X, Xn, c.o_n["nx"], N_, Nn, D, presummed=True)
                for i in range(KC):
                    s_ = wsq.next()
                    P = PB[i % 2]; Pn = "PB%d" % (i % 2)
                    mmg(P[:, :], [(wl(s_, k), N_(k)) for k in range(KC)], ["WS%d" % s_] + [Nn(k) for k in range(KC)], [Pn])
                    cp("act", H_(i), P[:, :], [Pn], [Hn(i)])
                Nall = [Nn(k) for k in range(KC)]
                dma("sp", XNf[:, 0:2 * KC * NM], kvs_v[:, :], ["kvsK", "kvsV"], Nall, "XN")
                KT = lambda j: XNf[:, j * NM:(j + 1) * NM]
                VT = lambda mb, j: XNf[:, KC * NM + mb * D + j * 128:KC * NM + mb * D + (j + 1) * 128]
                XH, DHC = c.XH, c.DHC
                sc_scale = float(c.DH ** -0.5)
                for hd in range(XH):
                    def attn_blk(blk, hd=hd):
                        bs = slice(blk * 128, blk * 128 + 128)
                        Psn = "PB%d" % (2 + blk % 2); Ps = PB[2 + blk % 2]
                        SMX = SMX2[:, 4 * (blk % 2):4 * (blk % 2) + 4]; SMn = "SMX%d" % (blk % 2)
                        mmg(Ps[:, 0:NM], [(HD[:, hd * DHC + dch, bs], KT(hd * DHC + dch)) for dch in range(DHC)],
                            [Hn(hd * DHC + dch) for dch in range(DHC)] + Nall, [Psn])
                        S.op("dve", lambda e, ctx, Ps=Ps: e.reduce_max(SMX[:, 0:1], Ps[:, 0:NM], mybir.AxisListType.X),
                             r=[Psn], w=[SMn])
                        ts("dve", SMX[:, 1:2], SMX[:, 0:1], -sc_scale, None, ALU.mult, None, [SMn], [SMn])
                        ex, exn = TMP[:, blk % 4, 0:NM], "TMP%d" % (blk % 4)
                        act(ex, Ps[:, 0:NM], AF.Exp, [Psn, SMn], [exn, SMn], bias=SMX[:, 1:2], scale=sc_scale, accum=SMX[:, 2:3])
                        S.op("dve", lambda e, ctx: e.reciprocal(SMX[:, 3:4], SMX[:, 2:3]), r=[SMn], w=[SMn])
                        ts("dve", PRB[:, blk % 2, :], ex, SMX[:, 3:4], None, ALU.mult, None, [exn, SMn], ["PRB%d" % (blk % 2)])
                        for mb in range(NM // 128):
                            tr(PT[:, (blk % 2) * 256 + mb * 128:(blk % 2) * 256 + (mb + 1) * 128],
                               PRB[:, blk % 2, mb * 128:(mb + 1) * 128], ["PRB%d" % (blk % 2)], ["PT"])
                        for mb in range(NM // 128):
                            cp("act", PNT[:, mb, bs], PT[:, (blk % 2) * 256 + mb * 128:(blk % 2) * 256 + (mb + 1) * 128],
                               ["PT"], ["PNT"])
                    for blk in range(0, TT // 128, 2):
                        la = S.record(lambda: attn_blk(blk))
                        lb_ = S.record(lambda: attn_blk(blk + 1))
                        S.play(la, lb_)
                    for dch in range(DHC):
                        j = hd * DHC + dch
                        P = PB[dch % 2]; Pn = "PB%d" % (dch % 2)
                        mmg(P[:, :], [(VT(mb, j), PNT[:, mb, :]) for mb in range(NM // 128)], Nall + ["PNT"], [Pn])
                        cp("act", H_(j), P[:, :], [Pn], [Hn(j)])
                wsq = WStream([("wo", j, 0) for j in range(KC)])
                for i in range(KC):
                    s_ = wsq.next()
                    P = PB[i % 2]; Pn = "PB%d" % (i % 2)
                    mmg(P[:, :], [(wl(s_, k), H_(k)) for k in range(KC)], ["WS%d" % s_] + [Hn(k) for k in range(KC)], [Pn])
                    tt("dve", X(i), P[:, :], X(i), ALU.add, [Pn, Xn(i)], [Xn(i)])
                    sumsq_push(i, KC, TT, X, Xn)
                sumsq_flush()
                ffn("f2gu", "f2d", c.o_n["n2"], presummed=True)
                SUM = PB[6]
                act(RS[:, :], SUM[:, :], AF.Sqrt, ["PB6"], ["RS"], bias=float(D * EPS), scale=1.0)
                S.op("dve", lambda e, ctx: e.reciprocal(RS[:, :], RS[:, :]), r=["RS"], w=["RS"])
                for i in range(KC):
                    w_ = NWS[:, c.o_n["nfin"] + i:c.o_n["nfin"] + i + 1]
                    stt("dve", X(i), X(i), w_, RS[:, :], ALU.mult, ALU.mult, [Xn(i), "NWS", "RS"], [Xn(i)])
                dma("sp", out_v[:, :, t0:t0 + TT], XR[:, 0:KC, :], [Xn(i) for i in range(KC)], ["out"], "outst")

        S.op("sp", lambda e, ctx: None, r=["out", "dbg"], w=[])
        S.finalize(nc, None)
        sems = {}
        for k in S.maxcnt:
            sems[k] = es.enter_context(nc.semaphore("s_%s_%s" % k))
        with nc.Block() as block:
            S.emit(nc, block, sems)
    return nc, S


def _tiles(W, D):
    K, N = W.shape
    KC = D // 128
    a = W.reshape(K // D, KC, 128, N // 128, 128)
    a = a.transpose(0, 3, 2, 1, 4)
    return np.ascontiguousarray(a).reshape(K // D, N // 128, 128, D)


def _fm(v, KC):
    return np.ascontiguousarray(v.reshape(KC, 128).T)


def make_inputs(cfg, x, mem, ffn1_norm, ffn1_w_gate, ffn1_w_up, ffn1_w_down, mix_norm, w_in, conv_w, conv_b,
                lru_w_a, lru_b_a, lru_w_x, lru_b_x, lru_lambda, hg_lb_logits, hg_gnorm, w_out,
                xattn_norm, mem_norm, xattn_w_q, xattn_w_k, xattn_w_v, xattn_w_o,
                ffn2_norm, ffn2_w_gate, ffn2_w_up, ffn2_w_down, final_norm):
    c = cfg
    D, KC, NF, NS, T, LC, HC, YC, LW, HW = c.D, c.KC, c.NF, c.NS, c.T, c.LC, c.HC, c.YC, c.LW, c.HW
    f32 = np.float32
    A = lambda a: np.asarray(a, dtype=f32)
    groups = {}

    def gu(wgt, wup):
        tg = _tiles(A(wgt)[0], D)[0]
        tu = _tiles(A(wup)[0], D)[0]
        out = np.empty((2 * NF, 128, D), f32)
        out[0::2] = tg; out[1::2] = tu
        return out
    groups["f1gu"] = gu(ffn1_w_gate, ffn1_w_up)
    groups["f1d"] = _tiles(A(ffn1_w_down)[0], D).reshape(NS * KC, 128, D)
    groups["f2gu"] = gu(ffn2_w_gate, ffn2_w_up)
    groups["f2d"] = _tiles(A(ffn2_w_down)[0], D).reshape(NS * KC, 128, D)
    wi = _tiles(A(w_in)[0], D)[0]
    order = []
    for ch in range(LC):
        order.append(ch)
        order.append(LW // 128 + ch)
    for hd in range(HC):
        for sec in range(4):
            order.append(2 * LW // 128 + sec * (HW // 128) + hd)
    groups["win"] = wi[order]
    groups["wout"] = _tiles(A(w_out)[0], D)[0]
    groups["wq"] = _tiles(A(xattn_w_q)[0], D)[0]
    groups["wk"] = _tiles(A(xattn_w_k)[0], D)[0]
    groups["wv"] = _tiles(A(xattn_w_v)[0], D)[0]
    groups["wo"] = _tiles(A(xattn_w_o)[0], D)[0]
    wsh = np.concatenate([groups[gn] for gn, cnt in c.groups], axis=0).reshape(c.ntile8 * 128, D)
    cf = np.zeros((128, c.ncf), f32)
    cf[:, c.o_id:c.o_id + 128] = np.eye(128, dtype=f32)
    s_ = np.arange(128)[:, None]; t_ = np.arange(128)[None, :]
    cf[:, c.o_mk:c.o_mk + 128] = ((s_ // 64 == t_ // 64) & (s_ <= t_)).astype(f32)
    cm = np.ones(TT, f32); cm[0::64] = 0.0
    cf[:, c.o_cm:c.o_cm + TT] = cm[None, :]
    x = A(x); mem = A(mem)
    cw_ = A(conv_w)[0][:, 0, :]
    sm = np.zeros((128, c.nsm), f32)
    for nm, v in (("n1", ffn1_norm), ("nmix", mix_norm), ("nx", xattn_norm), ("nmem", mem_norm), ("n2", ffn2_norm)):
        sm[:, c.o_n[nm]:c.o_n[nm] + KC] = _fm(A(v)[0], KC)
    sm[:, c.o_n["nfin"]:c.o_n["nfin"] + KC] = _fm(A(final_norm), KC)
    for ch in range(LC):
        sl = slice(ch * 128, (ch + 1) * 128)
        for j in range(4):
            sm[:, c.o_cw + ch * 4 + j] = cw_[j, sl]
        sm[:, c.o_cb + ch] = A(conv_b)[0][sl]
        sm[:, c.o_ba + ch] = A(lru_b_a)[0][ch]
        sm[:, c.o_bx + ch] = A(lru_b_x)[0][ch]
        sm[:, c.o_lam + ch] = A(lru_lambda)[0][sl]
    for hd in range(HC):
        sm[:, c.o_lb0 + hd] = A(hg_lb_logits)[0][hd * 128:(hd + 1) * 128]
        sm[:, c.o_lb1 + hd] = A(hg_lb_logits)[1][hd * 128:(hd + 1) * 128]
    sm[:, c.o_gn] = A(hg_gnorm)[0]
    gw = np.empty((128, 2, LC, 128), f32)
    for ch in range(LC):
        gw[:, 0, ch, :] = A(lru_w_a)[0][ch]
        gw[:, 1, ch, :] = A(lru_w_x)[0][ch]
    gw = gw.reshape(128, 2 * LC * 128)
    in_maps = []
    if not c.split:
        for core in range(x.shape[0]):
            in_maps.append({
                "xT": np.ascontiguousarray(x[core].T),
                "memT": np.ascontiguousarray(mem[core].T),
                "wsh": wsh, "smd": sm, "cfd": cf, "gwd": gw,
            })
        return in_maps
    for core in range(2 * x.shape[0]):
        b = core // 2; g = core % 2
        smc = sm.copy()
        smc[:, c.o_flag] = float(g)
        in_maps.append({
            "xT": np.ascontiguousarray(x[b, g * T:(g + 1) * T, :].T),
            "xP": np.ascontiguousarray(x[b, 0:T, :].T),
            "memT": np.ascontiguousarray(mem[b].T),
            "wsh": wsh, "smd": smc, "cfd": cf, "gwd": gw,
        })
    return in_maps


def kernel(**inputs):
    x = inputs["x"]
    B, S_, D = x.shape
    DFF = inputs["ffn1_w_gate"].shape[-1]
    cfg = Cfg(D, DFF, S_, inputs["mem"].shape[1], split=True)
    in_maps = make_inputs(cfg, **inputs)
    nc, _ = build(cfg)
    ncore = len(in_maps)
    res = run_bass_kernel_spmd(nc, in_maps, core_ids=list(range(ncore)))
    out = np.empty((B, S_, D), np.float32)
    T = cfg.T
    for core in range(ncore):
        b = core // 2; g = core % 2
        out[b, g * T:(g + 1) * T, :] = res.results[core]["outT"].T
    return out
```

```python
import os
import numpy as np
import concourse.bass as bass
import concourse.mybir as mybir
from concourse.bass_utils import run_bass_kernel_spmd

F32 = mybir.dt.float32
BF16 = mybir.dt.bfloat16
AF = mybir.ActivationFunctionType
ALU = mybir.AluOpType
EPS = 1e-6
SAME_ENGINE_SYNC = True
NOSYNC_ENGS = set(os.environ.get('NOSYNC', '').split(',')) - {''}
NW = 4
TT = 512


class Cfg:
    def __init__(self, D, DFF, S, NM=256, split=True):
        self.split = split
        self.D = D; self.KC = D // 128; self.DFF = DFF; self.NF = DFF // 128; self.NS = DFF // D
        self.S = S; self.T = S // 2 if split else S; self.NTO = self.T // TT; self.NTF = self.NTO
        self.NTP = self.NTO if split else 0
        self.LW = D // 2; self.HW = D // 2
        self.LCH = self.LW // 128; self.LC = self.LCH
        self.HH = self.HW // 128; self.HC = self.HH
        self.YC = self.LC + self.HC
        self.NM = NM; self.XH = 4; self.DH = D // 4; self.DHC = self.DH // 128
        self.WIN_HALF = 2 * self.LC + 4 * self.HC
        self.groups = [("f1gu", 2 * self.NF), ("f1d", self.NS * self.KC), ("win", self.WIN_HALF),
                       ("wout", self.KC), ("wq", self.KC), ("wk", self.KC), ("wv", self.KC), ("wo", self.KC),
                       ("f2gu", 2 * self.NF), ("f2d", self.NS * self.KC)]
        self.ntile8 = sum(c for _, c in self.groups)
        KC, LC, HC = self.KC, self.LC, self.HC
        o = 0
        self.o_n = {}
        for nm in ("n1", "nmix", "nx", "nmem", "n2", "nfin"):
            self.o_n[nm] = o; o += KC
        self.o_cw = o; o += LC * 4
        self.o_cb = o; o += LC
        self.o_ba = o; o += LC
        self.o_bx = o; o += LC
        self.o_lam = o; o += LC
        self.o_lb0 = o; o += HC
        self.o_lb1 = o; o += HC
        self.o_gn = o; o += 1
        self.o_flag = o; o += 1
        self.nsm = o
        self.o_id = 0; self.o_mk = 128; self.o_cm = 256; self.ncf = 256 + TT


class Op:
    __slots__ = ("idx", "eng", "fn", "deps", "dma", "signal", "val", "waits", "amt")

    def __init__(self, idx, eng, fn, deps, dma, amt):
        self.idx = idx; self.eng = eng; self.fn = fn; self.deps = deps; self.dma = dma
        self.signal = False; self.val = 0; self.waits = []; self.amt = amt


class Sched:
    ENGS = ("pe", "act", "dve", "pool", "sp")

    def __init__(self):
        self.ops = []
        self.lastw = {}
        self.readers = {}
        self.psum = set()
        self.rec = None

    def record(self, f):
        old = self.rec
        self.rec = []
        f()
        out = self.rec
        self.rec = old
        return out

    def play(self, *lists):
        lists = [l for l in lists if l]
        pos = [0] * len(lists)
        total = sum(len(l) for l in lists)
        for _ in range(total):
            k = min((i for i in range(len(lists)) if pos[i] < len(lists[i])), key=lambda i: pos[i] / len(lists[i]))
            self.op(*lists[k][pos[k]][:4], dma=lists[k][pos[k]][4], amt=lists[k][pos[k]][5])
            pos[k] += 1

    def op(self, eng, fn, r=(), w=(), dma=None, amt=16):
        if self.rec is not None:
            self.rec.append((eng, fn, tuple(r), tuple(w), dma, amt))
            return None
        deps = set()
        for b in r:
            x = self.lastw.get(b)
            if x is not None:
                deps.add(x)
            if b in self.psum:
                deps.update(y for y in self.readers.get(b, ()) if self.ops[y].eng != eng)
        for b in w:
            x = self.lastw.get(b)
            if x is not None:
                deps.add(x)
            deps.update(self.readers.get(b, ()))
        idx = len(self.ops)
        o = Op(idx, eng, fn, deps, dma, amt)
        self.ops.append(o)
        for b in r:
            self.readers.setdefault(b, []).append(idx)
        for b in w:
            self.lastw[b] = idx
            self.readers[b] = []
        return o

    def finalize(self, nc, sems):
        ops = self.ops
        for o in ops:
            best = {}
            for d in o.deps:
                dop = ops[d]
                if dop.dma is not None:
                    key = ("dma", dop.dma)
                else:
                    if dop.eng == o.eng and o.dma is None and (o.eng == "pe" or not SAME_ENGINE_SYNC or o.eng in NOSYNC_ENGS):
                        continue
                    key = ("eng", dop.eng)
                if key not in best or best[key] < d:
                    best[key] = d
            o.deps = best
            for d in best.values():
                ops[d].signal = True
        cnt = {}
        for o in ops:
            if o.dma is not None:
                k = ("dma", o.dma)
                cnt[k] = cnt.get(k, 0) + o.amt
                o.val = cnt[k]
                o.signal = True
            elif o.signal:
                k = ("eng", o.eng)
                cnt[k] = cnt.get(k, 0) + 1
                o.val = cnt[k]
        for o in ops:
            o.waits = [(k, ops[d].val) for k, d in o.deps.items()]
        self.maxcnt = cnt

    def emit(self, nc, block, sems):
        decos = {"pe": block.tensor, "act": block.scalar, "dve": block.vector, "pool": block.gpsimd, "sp": block.sync}
        for en in self.ENGS:
            eops = [o for o in self.ops if o.eng == en]

            def body(e, eops=eops):
                waited = {}
                ctx = {}
                for o in eops:
                    for k, v in o.waits:
                        if waited.get(k, 0) < v:
                            e.wait_ge(sems[k], v)
                            waited[k] = v
                    ins = o.fn(e, ctx)
                    if o.signal:
                        assert ins is not None
                        if o.dma is not None:
                            ins.then_inc(sems[("dma", o.dma)], o.amt)
                        else:
                            ins.then_inc(sems[("eng", o.eng)], 1)
            decos[en](body)


def build(cfg, dbg=None):
    c = cfg
    D, KC, NF, NS, T, NTO, NTF, LC, HC, YC, NM = c.D, c.KC, c.NF, c.NS, c.T, c.NTO, c.NTF, c.LC, c.HC, c.YC, c.NM
    nc = bass.Bass("TRN2", target_bir_lowering=False)
    xT = nc.dram_tensor("xT", [D, T], F32, kind="ExternalInput").ap()
    memT = nc.dram_tensor("memT", [D, NM], F32, kind="ExternalInput").ap()
    xP = nc.dram_tensor("xP", [D, T], F32, kind="ExternalInput").ap() if c.split else None
    wsh = nc.dram_tensor("wsh", [c.ntile8 * 128, D], F32, kind="ExternalInput").ap()
    smd = nc.dram_tensor("smd", [128, c.nsm], F32, kind="ExternalInput").ap()
    cfd = nc.dram_tensor("cfd", [128, c.ncf], F32, kind="ExternalInput").ap()
    gwd = nc.dram_tensor("gwd", [128, 2 * LC * 128], F32, kind="ExternalInput").ap()
    outT = nc.dram_tensor("outT", [D, T], F32, kind="ExternalOutput").ap()
    dbgo = None
    if dbg:
        dbgo = nc.dram_tensor("dbgo", [D, T], F32, kind="ExternalOutput").ap()
    wg = {}
    for gn, cnt in c.groups:
        wg[gn] = nc.dram_tensor("wg_" + gn, [cnt * 128, D], BF16)
    hs = nc.dram_tensor("hs", [D, T], F32)
    xn2i = nc.dram_tensor("xn2i", [D, T], BF16)
    yi = nc.dram_tensor("yi", [YC * 128, T], BF16)
    kvs = nc.dram_tensor("kvs", [128, 2 * KC * NM], BF16)

    S = Sched()
    S.psum = {"PB%d" % i for i in range(7)} | {"PT"}
    NXR = max(KC, 32)
    NHD = max(KC, 32)
    import contextlib
    es = contextlib.ExitStack()
    with es:
        def sb(name, shape, dt):
            return es.enter_context(nc.sbuf_tensor(name, shape, dt))

        def ps(name, shape, dt):
            return es.enter_context(nc.psum_tensor(name, shape, dt))
        XR = sb("XR", [128, NXR, TT], F32)
        XN = sb("XN", [128, KC, TT], BF16)
        HD = sb("HD", [128, NHD, TT], BF16)
        WS = sb("WS", [128, NW, D], BF16)
        TMP = sb("TMP", [128, 4, TT], F32)
        TMPB = sb("TMPB", [128, 4, TT], BF16)
        SM = sb("SM", [128, c.nsm], F32)
        CF = sb("CF", [128, c.ncf], F32)
        GWB = sb("GWB", [128, 2 * LC * 128], BF16)
        IDB = sb("IDB", [128, 128], BF16)
        MKB = sb("MKB", [128, 128], BF16)
        ONESB = sb("ONESB", [128, 128], BF16)
        RS = sb("RS", [128, TT], F32)
        NWS = sb("NWS", [128, 6 * KC], F32)
        SLAM = sb("SLAM", [128, LC], F32)
        LB = sb("LB", [128, 2 * HC], F32)
        GNW = sb("GNW", [128, 1], F32)
        LXW = sb("LXW", [128, 2, TT + 3], F32)
        HALO = sb("HALO", [128, LC, 3], F32)
        HCAR = sb("HCAR", [128, LC], F32)
        SST = sb("SST", [128, HC, 128], F32)
        SSB = sb("SSB", [128, HC, 128], BF16)
        EBL2 = sb("EBL", [128, 2, 8], F32)
        SMX2 = sb("SMX", [128, 16], F32)
        if HC * 128 >= 2 * TT + 2 * NM:
            SSBf = SSB[:, :, :].rearrange("p a b -> p (a b)")
            PNT = SSBf[:, 0:2 * TT].rearrange("p (a b) -> p a b", a=2)
            PRB = SSBf[:, 2 * TT:2 * TT + 2 * NM].rearrange("p (a b) -> p a b", a=2)
        else:
            PNT = sb("PNT", [128, 2, TT], BF16)
            PRB = sb("PRB", [128, 2, NM], BF16)
        PB = [ps("PB%d" % i, [128, TT], F32) for i in range(7)]
        PT = ps("PT", [128, 1024], BF16)

        cnt_names = {}

        def X(i):
            return XR[:, i, :]

        def N_(i):
            return XN[:, i, :]

        def H_(i):
            return HD[:, i, :]

        def dma(eng, out, in_, r, w, key):
            S.op(eng, lambda e, ctx, out=out, in_=in_: e.dma_start(out=out, in_=in_), r=r, w=w, dma=key)

        def act(out, in_, func, r, w, bias=None, scale=None, accum=None):
            kw = {}
            if bias is not None:
                kw["bias"] = bias
            if scale is not None:
                kw["scale"] = scale
            if accum is not None:
                kw["accum_out"] = accum
            S.op("act", lambda e, ctx: e.activation(out, in_, func, **kw), r=r, w=w)

        def tt(eng, out, a, b, op, r, w):
            S.op(eng, lambda e, ctx: e.tensor_tensor(out, a, b, op), r=r, w=w)

        def ts(eng, out, a, s1, s2, op0, op1, r, w):
            if op1 is None:
                S.op(eng, lambda e, ctx: e.tensor_scalar(out, a, s1, None, op0), r=r, w=w)
            else:
                S.op(eng, lambda e, ctx: e.tensor_scalar(out, a, s1, s2, op0, op1), r=r, w=w)

        def stt(eng, out, a, s, b, op0, op1, r, w):
            S.op(eng, lambda e, ctx: e.scalar_tensor_tensor(out, a, s, b, op0, op1), r=r, w=w)

        def cp(eng, out, in_, r, w):
            if eng == "act":
                S.op("act", lambda e, ctx: e.copy(out, in_), r=r, w=w)
            else:
                S.op(eng, lambda e, ctx: e.tensor_copy(out, in_), r=r, w=w)

        def mmg(out, pairs, r, w):
            n = len(pairs)

            def fn(e, ctx):
                ins = None
                for i, (l, rr) in enumerate(pairs):
                    ins = e.matmul(out, l, rr, start=(i == 0), stop=(i == n - 1))
                return ins
            S.op("pe", fn, r=r, w=w)

        def mm1(out, l, rr, start, stop, r, w):
            S.op("pe", lambda e, ctx: e.matmul(out, l, rr, start=start, stop=stop, skip_group_check=True), r=r, w=w)

        def tr(out, in_, r, w):
            S.op("pe", lambda e, ctx: e.transpose(out, in_, IDB[:, :]), r=r + ["IDB"], w=w)

        wstate = {"n": 0, "c": 0}

        class WStream:
            def __init__(self, tiles, ring=None):
                self.tiles = tiles; self.issued = 0; self.cur = 0; self.slots = []; self.ring = ring; self.k = 0

            def _issue(self, k):
                gn, ti, dyn = self.tiles[k]
                if self.ring is None:
                    slot = wstate["n"] % NW
                    wstate["n"] += 1
                else:
                    slot = self.ring[self.k % len(self.ring)]
                    self.k += 1
                self.slots.append(slot)
                dst = WS[:, slot, :]
                src = wg[gn][ti * 128:(ti + 1) * 128, :]
                dma("sp", dst, src, ["wg_%s_%d" % (gn, ti // 16)], ["WS%d" % slot], "WS%d" % slot)

            def next(self):
                depth = NW if self.ring is None else len(self.ring) + 1
                while self.issued < len(self.tiles) and self.issued < self.cur + depth - 1:
                    self._issue(self.issued); self.issued += 1
                if self.issued <= self.cur:
                    self._issue(self.issued); self.issued += 1
                slot = self.slots[self.cur]
                self.cur += 1
                return slot

        def wl(slot, kc):
            return WS[:, slot, kc * 128:(kc + 1) * 128]

        pbrot = {"i": 0}

        dma("sp", SM[:, :], smd[:, :], [], ["SM"], "SM")
        dma("sp", CF[:, :], cfd[:, :], [], ["CF"], "CF")
        dma("pool", GWB[:, :], gwd[:, :], [], ["GWB"], "GWB")
        cp("dve", IDB[:, :], CF[:, c.o_id:c.o_id + 128], ["CF"], ["IDB"])
        cp("dve", MKB[:, :], CF[:, c.o_mk:c.o_mk + 128], ["CF"], ["MKB"])
        S.op("dve", lambda e, ctx: e.memset(ONESB[:, :], 1.0), w=["ONESB"])
        S.op("dve", lambda e, ctx: e.memset(HALO[:, :, :], 0.0), w=["HALO%d" % i for i in range(LC)])
        S.op("dve", lambda e, ctx: e.memset(HCAR[:, :], 0.0), w=["HCAR%d" % i for i in range(LC)])
        S.op("dve", lambda e, ctx: e.memset(SST[:, :, :], 0.0), w=["SST%d" % i for i in range(HC)])
        S.op("dve", lambda e, ctx: e.memset(SSB[:, :, :], 0.0), w=["SSB%d" % i for i in range(HC)])
        ts("dve", NWS[:, :], SM[:, 0:6 * KC], float(np.sqrt(D)), None, ALU.mult, None, ["SM"], ["NWS"])
        act(SLAM[:, :], SM[:, c.o_lam:c.o_lam + LC], AF.Exp, ["SM"], ["SLAM"], scale=-1.0)
        act(SLAM[:, :], SLAM[:, :], AF.Ln, ["SLAM"], ["SLAM"], bias=1.0)
        ts("dve", SLAM[:, :], SLAM[:, :], -8.0, None, ALU.mult, None, ["SLAM"], ["SLAM"])
        tt("dve", LB[:, 0:HC], SM[:, c.o_lb0:c.o_lb0 + HC], SM[:, c.o_lb1:c.o_lb1 + HC], ALU.subtract, ["SM"], ["LB"])
        act(LB[:, 0:HC], LB[:, 0:HC], AF.Sigmoid, ["LB"], ["LB"])
        ts("dve", LB[:, HC:2 * HC], LB[:, 0:HC], -1.0, 1.0, ALU.mult, ALU.add, ["LB"], ["LB"])
        ts("dve", GNW[:, :], SM[:, c.o_gn:c.o_gn + 1], float(np.sqrt(128.0)), None, ALU.mult, None, ["SM"], ["GNW"])

        CH = 16
        grow = {}
        row = 0
        for gn, cnt in c.groups:
            grow[gn] = row; row += cnt
        cast_plan = []
        gcnt = dict(c.groups)
        f1 = []
        per_slab_gu = 2 * KC
        for s_ in range(NS):
            for k0 in range(s_ * per_slab_gu, (s_ + 1) * per_slab_gu, CH):
                f1.append(("f1gu", k0, min(k0 + CH, (s_ + 1) * per_slab_gu)))
            for k0 in range(s_ * KC, (s_ + 1) * KC, CH):
                f1.append(("f1d", k0, min(k0 + CH, (s_ + 1) * KC)))
        rest = []
        for gn, cnt in c.groups:
            if gn in ("f1gu", "f1d"):
                continue
            for k0 in range(0, cnt, CH):
                rest.append((gn, k0, min(cnt, k0 + CH)))
        cast_state = {"rest": rest}

        def cast_emit(chunks, extra_r=()):
            for gn, k0, k1 in chunks:
                src = wsh[(grow[gn] + k0) * 128:(grow[gn] + k1) * 128, :]
                kk = wstate["c"] % 4
                wstate["c"] += 1
                dma("pool", wg[gn][k0 * 128:k1 * 128, :], src, list(extra_r), ["wg_%s_%d" % (gn, k0 // CH), "wcastbuf%d" % kk],
                    "wcast%d" % kk)

        def cast_some(n, extra_r=()):
            r_ = cast_state["rest"]
            cast_emit(r_[:n], extra_r)
            cast_state["rest"] = r_[n:]
        first_src = (xP if c.NTP else xT).rearrange("(kc p) t -> p kc t", p=128)[:, :, 0:TT]
        preload = dbg is None
        if preload:
            dma("sp", XR[:, 0:KC, :], first_src, [], ["X%d" % i for i in range(KC)], "XR")
        if dbg != "w0":
            cast_emit(f1[0:1], ["X0"] if preload else [])
            cast_emit(f1[1:])
            if c.NTP:
                cast_emit([x_ for x_ in cast_state["rest"] if x_[0] == "win"])
                cast_state["rest"] = [x_ for x_ in cast_state["rest"] if x_[0] != "win"]

        sqs = {"i": 0}

        def sumsq_chunk(i, nchunks, width, src, srcn):
            SUM = PB[6]
            k = sqs["i"]; sqs["i"] += 1
            t_ = TMP[:, 2 + k % 2, 0:width]; tn = "TMP%d" % (2 + k % 2)
            act(t_, src(i), AF.Square, [srcn(i)], [tn])
            hi = TMPB[:, (2 * k) % 4, 0:width]; hin = "TMPB%d" % ((2 * k) % 4)
            lo = TMPB[:, (2 * k + 1) % 4, 0:width]; lon = "TMPB%d" % ((2 * k + 1) % 4)
            cp("dve", hi, t_, [tn], [hin])
            tt("dve", lo, t_, hi, ALU.subtract, [tn, hin], [lon])
            mm1(SUM[:, 0:width], ONESB[:, :], hi, i == 0, False, ["ONESB", hin], ["PB6"])
            mm1(SUM[:, 0:width], ONESB[:, :], lo, False, i == nchunks - 1, ["ONESB", lon], ["PB6"])

        def sumsq(nchunks, width, src, srcn):
            for i in range(nchunks):
                sumsq_chunk(i, nchunks, width, src, srcn)

        sq_pend = []
        SQ_LAG = 2

        def sumsq_push(i, nchunks, width, src, srcn):
            sq_pend.append((i, nchunks, width, src, srcn))
            if len(sq_pend) > SQ_LAG:
                sumsq_chunk(*sq_pend.pop(0))

        def sumsq_flush():
            while sq_pend:
                sumsq_chunk(*sq_pend.pop(0))

        def rmsnorm(nchunks, width, src, srcn, wcol, dst, dstn, dmodel, presummed=False):
            SUM = PB[6]
            if not presummed:
                sumsq(nchunks, width, src, srcn)
            act(RS[:, 0:width], SUM[:, 0:width], AF.Sqrt, ["PB6"], ["RS"], bias=float(dmodel * EPS), scale=1.0)
            S.op("dve", lambda e, ctx: e.reciprocal(RS[:, 0:width], RS[:, 0:width]), r=["RS"], w=["RS"])
            for i in range(nchunks):
                stt("dve", dst(i), src(i), NWS[:, wcol + i:wcol + i + 1], RS[:, 0:width], ALU.mult, ALU.mult,
                    [srcn(i), "NWS", "RS"], [dstn(i)])

        Xn = lambda i: "X%d" % i
        Nn = lambda i: "N%d" % i
        Hn = lambda i: "H%d" % i

        def ffn(gu, dn, wcol, presummed=False):
            rmsnorm(KC, TT, X, Xn, wcol, N_, Nn, D, presummed)
            tiles = []
            for s_ in range(NS):
                for j in range(KC):
                    jj = s_ * KC + j
                    tiles.append((gu, 2 * jj, 0)); tiles.append((gu, 2 * jj + 1, 0))
                for i in range(KC):
                    tiles.append((dn, s_ * KC + i, 0))
            wsq = WStream(tiles)
            for s_ in range(NS):
                for j in range(KC):
                    sg = wsq.next(); su = wsq.next()
                    G = PB[(2 * j) % 4]; U = PB[(2 * j + 1) % 4]
                    Gn = "PB%d" % ((2 * j) % 4); Un = "PB%d" % ((2 * j + 1) % 4)
                    mmg(G[:, :], [(wl(sg, k), N_(k)) for k in range(KC)], ["WS%d" % sg] + [Nn(k) for k in range(KC)], [Gn])
                    mmg(U[:, :], [(wl(su, k), N_(k)) for k in range(KC)], ["WS%d" % su] + [Nn(k) for k in range(KC)], [Un])
                    t_ = TMP[:, j % 2, :]
                    act(t_, G[:, :], AF.Silu, [Gn], ["TMP%d" % (j % 2)])
                    tt("dve", H_(j), t_, U[:, :], ALU.mult, ["TMP%d" % (j % 2), Un], [Hn(j)])
                for i in range(KC):
                    sd = wsq.next()
                    Dp = PB[4 + (i % 2)]; Dn = "PB%d" % (4 + (i % 2))
                    mmg(Dp[:, :], [(wl(sd, k), H_(k)) for k in range(KC)], ["WS%d" % sd] + [Hn(k) for k in range(KC)], [Dn])
                    stt("dve", X(i), Dp[:, :], 0.5, X(i), ALU.mult, ALU.add, [Dn, Xn(i)], [Xn(i)])
                    if s_ == NS - 1:
                        sumsq_push(i, KC, TT, X, Xn)
            sumsq_flush()

        xT_v = xT.rearrange("(kc p) t -> p kc t", p=128)
        hs_v = hs.ap().rearrange("(kc p) t -> p kc t", p=128)
        out_v = outT.rearrange("(kc p) t -> p kc t", p=128)
        xn2i_v = xn2i.ap().rearrange("(kc p) t -> p kc t", p=128)
        yi_v = yi.ap().rearrange("(q p) t -> p q t", p=128)
        dbg_v = dbgo.rearrange("(kc p) t -> p kc t", p=128) if dbg else None

        if True:
            do_l = dbg != "p2h"; do_h = dbg != "p2l"
            CM = CF[:, c.o_cm:c.o_cm + TT]
            wtp = {"i": 0}

            def WT():
                i = wtp["i"] % NXR; wtp["i"] += 1
                return XR[:, i, :], "X%d" % i
            wbp = {"i": 0}

            def WB():
                i = wbp["i"] % NHD; wbp["i"] += 1
                return HD[:, i, :], "H%d" % i

            def ystore(yt, ytn, q, tf):
                dma("sp", yi_v[:, q, tf * TT:(tf + 1) * TT], yt, [ytn], ["yi%d" % tf, "ystchain"], "yst")
            prj = {"i": 0}

            def PJ(p=0):
                if prj.get("mode") == "lru":
                    k = prj.get(p, 0); prj[p] = k + 1
                    i = 2 * p + k % 2
                else:
                    i = p
                return PB[i], "PB%d" % i

            def mixer_tile(tf, state_only):
                if not state_only:
                    dma("sp", XN[:, :, :], xn2i_v[:, :, tf * TT:(tf + 1) * TT], ["xn2i%d" % tf], [Nn(i) for i in range(KC)], "XN")

                def want(j):
                    if j < 2 * LC:
                        return do_l and (not state_only or j % 2 == 0)
                    return do_h and (not state_only or (j - 2 * LC) % 4 in (1, 2))
                def stream_of(j):
                    return (j // 2) % 2 if j < 2 * LC else ((j - 2 * LC) // 4) % 2
                wsqs = [WStream([("win", j, 0) for j in range(c.WIN_HALF) if want(j) and stream_of(j) == p_],
                                ring=[2 * p_, 2 * p_ + 1]) for p_ in range(2)]
                Nall = [Nn(k) for k in range(KC)]

                def proj(p=0):
                    s_ = wsqs[p].next()
                    P, Pn = PJ(p)
                    mmg(P[:, :], [(wl(s_, k), N_(k)) for k in range(KC)], ["WS%d" % s_] + Nall, [Pn])
                    return P, Pn
                def lru_chunk(ch):
                    sp_ = ch % 2
                    P, Pn = proj(sp_)
                    Ln = "LXW%d" % (ch % 2)
                    LX = LXW[:, ch % 2, :]
                    cp("dve", LX[:, 0:3], HALO[:, ch, :], ["HALO%d" % ch], [Ln])
                    cp("act", LX[:, 3:TT + 3], P[:, :], [Pn], [Ln])
                    cp("dve", HALO[:, ch, :], LX[:, TT:TT + 3], [Ln], ["HALO%d" % ch])
                    xc, xcn = WT()
                    cw = lambda j, ch=ch: SM[:, c.o_cw + ch * 4 + j:c.o_cw + ch * 4 + j + 1]
                    ts("dve", xc, LX[:, 0:TT], cw(0), SM[:, c.o_cb + ch:c.o_cb + ch + 1], ALU.mult, ALU.add, [Ln, "SM"], [xcn])
                    for j in (1, 2, 3):
                        stt("dve", xc, LX[:, j:j + TT], cw(j), xc, ALU.mult, ALU.add, [Ln, "SM", xcn], [xcn])
                    xcb, xcbn = WB()
                    cp("pool", xcb, xc, [xcn], [xcbn])
                    Pr, Prn = PJ(sp_)
                    mmg(Pr[:, :], [(GWB[:, (0 * LC + ch) * 128:(0 * LC + ch + 1) * 128], xcb)], ["GWB", xcbn], [Prn])
                    Pi, Pin = PJ(sp_)
                    mmg(Pi[:, :], [(GWB[:, (1 * LC + ch) * 128:(1 * LC + ch + 1) * 128], xcb)], ["GWB", xcbn], [Pin])
                    rr, rrn = WT()
                    act(rr, Pr[:, :], AF.Sigmoid, [Prn, "SM"], [rrn], bias=SM[:, c.o_ba + ch:c.o_ba + ch + 1])
                    ig, ign = WT()
                    act(ig, Pi[:, :], AF.Sigmoid, [Pin, "SM"], [ign], bias=SM[:, c.o_bx + ch:c.o_bx + ch + 1])
                    act(rr, rr, AF.Exp, [rrn, "SLAM"], [rrn], scale=SLAM[:, ch:ch + 1])
                    a2, a2n = WT()
                    tt("pool", a2, rr, rr, ALU.mult, [rrn], [a2n])
                    act(a2, a2, AF.Sqrt, [a2n], [a2n], bias=1.0, scale=-1.0)
                    tt("pool", ig, ig, xc, ALU.mult, [ign, xcn], [ign])
                    tt("dve", ig, ig, a2, ALU.mult, [ign, a2n], [ign])
                    hh, hhn = WT()
                    S.op("dve", lambda e, ctx, hh=hh, rr=rr, ig=ig, ch=ch: e.tensor_tensor_scan(
                        hh, rr, ig, HCAR[:, ch:ch + 1], ALU.mult, ALU.add),
                        r=[rrn, ign, "HCAR%d" % ch], w=[hhn])
                    cp("dve", HCAR[:, ch:ch + 1], hh[:, TT - 1:TT], [hhn], ["HCAR%d" % ch])
                    if state_only:
                        return
                    P, Pn = proj(sp_)
                    gs, gsn = WT()
                    cp("act", gs, P[:, :], [Pn], [gsn])
                    g2, g2n = WT()
                    tt("pool", g2, gs, gs, ALU.mult, [gsn], [g2n])
                    ts("pool", g2, g2, 0.044715, 1.0, ALU.mult, ALU.add, [g2n], [g2n])
                    tt("pool", g2, g2, gs, ALU.mult, [g2n, gsn], [g2n])
                    act(g2, g2, AF.Sigmoid, [g2n], [g2n], scale=1.5957691216057308)
                    tt("pool", g2, g2, gs, ALU.mult, [g2n, gsn], [g2n])
                    yt, ytn = WB()
                    tt("dve", yt, hh, g2, ALU.mult, [hhn, g2n], [ytn])
                    ystore(yt, ytn, ch, tf)
                prj["mode"] = "lru"
                for ch in range(0, LC if do_l else 0, 2):
                    la = S.record(lambda: lru_chunk(ch))
                    lb_ = S.record(lambda: lru_chunk(ch + 1)) if ch + 1 < LC else []
                    S.play(la, lb_)
                def head(hd):
                    sp_ = hd % 2
                    EBL = EBL2[:, hd % 2, :]; EBn = "EBL%d" % (hd % 2)
                    if not state_only:
                        Pq, Pqn = proj(sp_)
                        q, qn = WT()
                        act(q, Pq[:, :], AF.Silu, [Pqn], [qn])
                    Pf, Pfn = proj(sp_)
                    f, fn_ = WT()
                    act(f, Pf[:, :], AF.Sigmoid, [Pfn], [fn_])
                    ts("dve", f, f, LB[:, HC + hd:HC + hd + 1], LB[:, hd:hd + 1], ALU.mult, ALU.add, [fn_, "LB"], [fn_])
                    lf, lfn = WT()
                    act(lf, f, AF.Ln, [fn_], [lfn])
                    ts("pool", f, f, -1.0, 1.0, ALU.mult, ALU.add, [fn_], [fn_])
                    bc, bcn = WT()
                    S.op("dve", lambda e, ctx, bc=bc, lf=lf: e.tensor_tensor_scan(bc, CM, lf, 0.0, ALU.mult, ALU.add),
                         r=["CF", lfn], w=[bcn])
                    S.op("act", lambda e, ctx, bc=bc: e.activation(
                        EBL[:, 0:TT // 64], bc.rearrange("p (c s) -> p c s", s=64)[:, :, 63], AF.Exp),
                        r=[bcn], w=[EBn])
                    if not state_only:
                        act(lf, bc, AF.Exp, [bcn], [lfn])
                        qd, qdn = WB()
                        tt("dve", qd, q, lf, ALU.mult, [qn, lfn], [qdn])
                    act(bc, bc, AF.Exp, [bcn], [bcn], scale=-1.0)
                    if not state_only:
                        kd, kdn = WB()
                        tt("pool", kd, f, bc, ALU.mult, [fn_, bcn], [kdn])
                    kd2, kd2n = WB()
                    tt("pool", bc, f, bc, ALU.mult, [fn_, bcn], [bcn])
                    NCH = TT // 64
                    tt("dve", kd2.rearrange("p (c s) -> p c s", s=64), bc.rearrange("p (c s) -> p c s", s=64),
                       EBL[:, 0:NCH].unsqueeze(2).broadcast_to([128, NCH, 64]), ALU.mult, [bcn, EBn], [kd2n])
                    Pv, Pvn = proj(sp_)
                    vb, vbn = WB()
                    cp("act", vb, Pv[:, :], [Pvn], [vbn])
                    if not state_only:
                        Pg, Pgn = proj(sp_)
                        gg, ggn = WT()
                        act(gg, Pg[:, :], AF.Silu, [Pgn], [ggn])
                    HCUT = int(os.environ.get("HCUT", "9"))
                    if HCUT < 1:
                        return
                    kT, kTn = WB()
                    vT, vTn = WB()
                    pt0 = sp_ * 512
                    for blk in range(TT // 128):
                        bs = slice(blk * 128, blk * 128 + 128)
                        tr(PT[:, pt0 + blk * 128:pt0 + (blk + 1) * 128], kd2[:, bs], [kd2n], ["PT"])
                    cp("dve", kT, PT[:, pt0:pt0 + 512], ["PT"], [kTn])
                    for blk in range(TT // 128):
                        bs = slice(blk * 128, blk * 128 + 128)
                        tr(PT[:, pt0 + blk * 128:pt0 + (blk + 1) * 128], vb[:, bs], [vbn], ["PT"])
                    cp("act", vT, PT[:, pt0:pt0 + 512], ["PT"], [vTn])
                    PO = PB[2 + hd % 2]; POn = "PB%d" % (2 + hd % 2); SC = PB[4]; DS = PB[5]
                    if HCUT < 2:
                        return
                    for blk in range(TT // 128):
                        bs = slice(blk * 128, blk * 128 + 128)
                        if not state_only:
                            scs = slice(sp_ * 256 + (blk % 2) * 128, sp_ * 256 + (blk % 2) * 128 + 128)
                            mm1(SC[:, scs], kd[:, bs], qd[:, bs], True, True, [kdn, qdn], ["PB4"])
                            scm, scmn = WB()
                            tt("dve", scm[:, 0:128], SC[:, scs], MKB[:, :], ALU.mult, ["PB4", "MKB"], [scmn])
                            mm1(PO[:, bs], vT[:, bs], scm[:, 0:128], True, False, [vTn, scmn], [POn])
                        for hf in range(2 if HCUT >= 3 else 0):
                            cc = blk * 2 + hf
                            cs = slice(cc * 64, cc * 64 + 64)
                            pr = slice(hf * 64, hf * 64 + 64)
                            if not state_only:
                                mm1(PO[:, cs], SSB[:, hd, :], qd[:, cs], False, hf == 1, ["SSB%d" % hd, qdn], [POn])
                            if HCUT < 4:
                                continue
                            dsl = slice(sp_ * 256 + (cc % 2) * 128, sp_ * 256 + (cc % 2) * 128 + 128)
                            mm1(DS[:, dsl], kT[pr, bs], vT[pr, bs], True, True, [kTn, vTn], ["PB5"])
                            stt("dve", SST[:, hd, :], SST[:, hd, :], EBL[:, cc:cc + 1], DS[:, dsl], ALU.mult, ALU.add,
                                ["SST%d" % hd, EBn, "PB5"], ["SST%d" % hd])
                            if not state_only:
                                cp("act", SSB[:, hd, :], SST[:, hd, :], ["SST%d" % hd], ["SSB%d" % hd])
                    if HCUT < 5 or state_only:
                        return
                    osq, osqn = WT()
                    act(osq, PO[:, :], AF.Square, [POn], [osqn])
                    ohi, ohin = WB()
                    cp("pool", ohi, osq, [osqn], [ohin])
                    olo, olon = WB()
                    tt("pool", olo, osq, ohi, ALU.subtract, [osqn, ohin], [olon])
                    mm1(PB[sp_][:, :], ONESB[:, :], ohi, True, False, ["ONESB", ohin], ["PB%d" % sp_])
                    mm1(PB[sp_][:, :], ONESB[:, :], olo, False, True, ["ONESB", olon], ["PB%d" % sp_])
                    act(osq, PB[sp_][:, :], AF.Sqrt, ["PB%d" % sp_], [osqn], bias=float(128 * EPS), scale=1.0)
                    S.op("dve", lambda e, ctx, osq=osq: e.reciprocal(osq, osq), r=[osqn], w=[osqn])
                    tt("dve", osq, PO[:, :], osq, ALU.mult, [POn, osqn], [osqn])
                    yt, ytn = WB()
                    stt("dve", yt, osq, GNW[:, 0:1], gg, ALU.mult, ALU.mult, [osqn, "GNW", ggn], [ytn])
                    ystore(yt, ytn, LC + hd, tf)
                prj["mode"] = "head"
                for hd in range(0, HC if do_h else 0, 2):
                    ha = S.record(lambda: head(hd))
                    hb = S.record(lambda: head(hd + 1)) if hd + 1 < HC else []
                    S.play(ha, hb)

        if dbg not in ("w0", "w1", "p1"):
            for t in range(c.NTP):
                t0 = t * TT
                if not (preload and t == 0):
                    dma("sp", XR[:, 0:KC, :], xP.rearrange("(kc p) t -> p kc t", p=128)[:, :, t0:t0 + TT], [], [Xn(i) for i in range(KC)], "XR")
                ffn("f1gu", "f1d", c.o_n["n1"])
                rmsnorm(KC, TT, X, Xn, c.o_n["nmix"], N_, Nn, D, presummed=True)
                mixer_tile(-1 - t, True)
                npace = max(1, c.NTP - 1)
                nrest = len(cast_state["rest"])
                if nrest:
                    cast_some(nrest if t >= npace - 1 else -(-nrest // (npace - t)), ["SST%d" % (HC - 1)])
            if c.NTP:
                fl = SM[:, c.o_flag:c.o_flag + 1]
                ts("dve", HCAR[:, :], HCAR[:, :], fl, None, ALU.mult, None, ["SM"] + ["HCAR%d" % i for i in range(LC)], ["HCAR%d" % i for i in range(LC)])
                HALOf = HALO[:, :, :].rearrange("p a b -> p (a b)")
                ts("dve", HALOf, HALOf, fl, None, ALU.mult, None, ["SM"] + ["HALO%d" % i for i in range(LC)], ["HALO%d" % i for i in range(LC)])
                SSTf = SST[:, :, :].rearrange("p a b -> p (a b)")
                ts("dve", SSTf, SSTf, fl, None, ALU.mult, None, ["SM"] + ["SST%d" % i for i in range(HC)], ["SST%d" % i for i in range(HC)])
                cp("act", SSB[:, :, :].rearrange("p a b -> p (a b)"), SSTf, ["SST%d" % i for i in range(HC)], ["SSB%d" % i for i in range(HC)])

        if dbg in ("w0", "w1"):
            dma("sp", XR[:, 0:KC, :], xT_v[:, :, 0:TT], [], [Xn(i) for i in range(KC)], "XR")
            rd = [Xn(i) for i in range(KC)] + ["SLAM", "LB", "GNW", "NWS", "IDB", "MKB", "GWB", "ONESB"]
            if dbg == "w1":
                rd += ["wg_%s_%d" % (gn, k // 16) for gn, cnt in c.groups for k in range(0, cnt, 16)]
            dma("sp", dbg_v[:, :, 0:TT], XR[:, 0:KC, :], rd, ["dbg"], "dbgst")
        for t in range(NTO if dbg not in ("w0", "w1") else 0):
            t0 = t * TT
            if not (preload and t == 0 and not c.NTP):
                dma("sp", XR[:, 0:KC, :], xT_v[:, :, t0:t0 + TT], [], [Xn(i) for i in range(KC)], "XR")
            ffn("f1gu", "f1d", c.o_n["n1"])
            dma("sp", hs_v[:, :, t0:t0 + TT], XR[:, 0:KC, :], [Xn(i) for i in range(KC)], ["hs%d" % t], "hsst")
            npace = max(1, NTO - 2)
            nrest = len(cast_state["rest"])
            if nrest:
                cast_some(nrest if t >= npace - 1 else -(-nrest // (npace - t)), ["hs%d" % t])
            rmsnorm(KC, TT, X, Xn, c.o_n["nmix"], N_, Nn, D, presummed=True)
            dma("sp", xn2i_v[:, :, t0:t0 + TT], XN[:, :, :], [Nn(i) for i in range(KC)], ["xn2i%d" % t], "xn2st")
            if dbg == "p1":
                dma("sp", dbg_v[:, :, t0:t0 + TT], XR[:, 0:KC, :], [Xn(i) for i in range(KC)], ["dbg"], "dbgst")


        if dbg not in ("p1", "w0", "w1"):
            for tf in range(NTF):
                mixer_tile(tf, False)
            if dbg in ("p2", "p2l", "p2h"):
                for t in range(NTO):
                    dma("sp", HD[:, 0:YC, :], yi_v[:, :, t * TT:(t + 1) * TT], ["yi%d" % t], [Hn(i) for i in range(YC)], "HDl")
                    for i in range(KC):
                        cp("dve", X(i), H_(i), [Hn(i)], [Xn(i)])
                    dma("sp", dbg_v[:, :, t * TT:(t + 1) * TT], XR[:, 0:KC, :], [Xn(i) for i in range(KC)], ["dbg"], "dbgst")

        if dbg is None or dbg == "p3":
            kvs_v = kvs.ap()
            dma("sp", XR[:, 0:KC, 0:NM], memT.rearrange("(kc p) m -> p kc m", p=128), [], [Xn(i) for i in range(KC)], "XR")
            rmsnorm(KC, NM, lambda i: XR[:, i, 0:NM], Xn, c.o_n["nmem"], lambda i: XN[:, i, 0:NM], Nn, D)
            wsq = WStream([("wk", j, 0) for j in range(KC)] + [("wv", j, 0) for j in range(KC)])
            for which in range(2):
                for j in range(KC):
                    s_ = wsq.next()
                    P = PB[j % 2]; Pn = "PB%d" % (j % 2)
                    mmg(P[:, 0:NM], [(wl(s_, k), XN[:, k, 0:NM]) for k in range(KC)], ["WS%d" % s_] + [Nn(k) for k in range(KC)], [Pn])
                    cp("act", HD[:, j, which * NM:(which + 1) * NM], P[:, 0:NM], [Pn], [Hn(j)])
            XNf = XN[:, :, :].rearrange("p a b -> p (a b)")
            for j in range(KC):
                for mb in range(NM // 128):
                    tr(PT[:, (j % 4) * 256 + mb * 128:(j % 4) * 256 + (mb + 1) * 128],
                       HD[:, j, NM + mb * 128:NM + (mb + 1) * 128], [Hn(j)], ["PT"])
                for mb in range(NM // 128):
                    cp("dve", XNf[:, mb * D + j * 128:mb * D + (j + 1) * 128],
                       PT[:, (j % 4) * 256 + mb * 128:(j % 4) * 256 + (mb + 1) * 128], ["PT"], [Nn(k) for k in range(KC)])
            dma("sp", kvs_v[:, 0:KC * NM].rearrange("p (j m) -> p j m", m=NM), HD[:, 0:KC, 0:NM], [Hn(j) for j in range(KC)], ["kvsK"], "kvst")
            dma("sp", kvs_v[:, KC * NM:2 * KC * NM], XNf[:, 0:2 * D], [Nn(k) for k in range(KC)], ["kvsV"], "kvst2")

            for t in range(NTO):
                t0 = t * TT
                dma("sp", XR[:, 0:KC, :], hs_v[:, :, t0:t0 + TT], ["hs%d" % t], [Xn(i) for i in range(KC)], "XR")

                dma("sp", HD[:, 0:YC, :], yi_v[:, :, t0:t0 + TT], ["yi%d" % t], [Hn(i) for i in range(YC)], "HDl")
                wsq = WStream([("wout", j, 0) for j in range(KC)] + [("wq", j, 0) for j in range(KC)])
                for i in range(KC):
                    s_ = wsq.next()
                    P = PB[i % 2]; Pn = "PB%d" % (i % 2)
                    mmg(P[:, :], [(wl(s_, k), H_(k)) for k in range(KC)], ["WS%d" % s_] + [Hn(k) for k in range(KC)], [Pn])
                    tt("dve", X(i), P[:, :], X(i), ALU.add, [Pn, Xn(i)], [Xn(i)])
                    sumsq_push(i, KC, TT, X, Xn)
                sumsq_flush()
                rmsnorm(KC, TT, X, Xn, c.o_n["nx"], N_, Nn, D, presummed=True)
                for i in range(KC):
                    s_ = wsq.next()
                    P = PB[i % 2]; Pn = "PB%d" % (i % 2)
                    mmg(P[:, :], [(wl(s_, k), N_(k)) for k in range(KC)], ["WS%d" % s_] + [Nn(k) for k in range(KC)], [Pn])
                    cp("act", H_(i), P[:, :], [Pn], [Hn(i)])
                Nall = [Nn(k) for k in range(KC)]
                dma("sp", XNf[:, 0:2 * KC * NM], kvs_v[:, :], ["kvsK", "kvsV"], Nall, "XN")
                KT = lambda j: XNf[:, j * NM:(j + 1) * NM]
                VT = lambda mb, j: XNf[:, KC * NM + mb * D + j * 128:KC * NM + mb * D + (j + 1) * 128]
                XH, DHC = c.XH, c.DHC
                sc_scale = float(c.DH ** -0.5)
                for hd in range(XH):
                    def attn_blk(blk, hd=hd):
                        bs = slice(blk * 128, blk * 128 + 128)
                        Psn = "PB%d" % (2 + blk % 2); Ps = PB[2 + blk % 2]
                        SMX = SMX2[:, 4 * (blk % 2):4 * (blk % 2) + 4]; SMn = "SMX%d" % (blk % 2)
                        mmg(Ps[:, 0:NM], [(HD[:, hd * DHC + dch, bs], KT(hd * DHC + dch)) for dch in range(DHC)],
                            [Hn(hd * DHC + dch) for dch in range(DHC)] + Nall, [Psn])
                        S.op("dve", lambda e, ctx, Ps=Ps: e.reduce_max(SMX[:, 0:1], Ps[:, 0:NM], mybir.AxisListType.X),
                             r=[Psn], w=[SMn])
                        ts("dve", SMX[:, 1:2], SMX[:, 0:1], -sc_scale, None, ALU.mult, None, [SMn], [SMn])
                        ex, exn = TMP[:, blk % 4, 0:NM], "TMP%d" % (blk % 4)
                        act(ex, Ps[:, 0:NM], AF.Exp, [Psn, SMn], [exn, SMn], bias=SMX[:, 1:2], scale=sc_scale, accum=SMX[:, 2:3])
                        S.op("dve", lambda e, ctx: e.reciprocal(SMX[:, 3:4], SMX[:, 2:3]), r=[SMn], w=[SMn])
                        ts("dve", PRB[:, blk % 2, :], ex, SMX[:, 3:4], None, ALU.mult, None, [exn, SMn], ["PRB%d" % (blk % 2)])
                        for mb in range(NM // 128):
                            tr(PT[:, (blk % 2) * 256 + mb * 128:(blk % 2) * 256 + (mb + 1) * 128],
                               PRB[:, blk % 2, mb * 128:(mb + 1) * 128], ["PRB%d" % (blk % 2)], ["PT"])
                        for mb in range(NM // 128):
                            cp("act", PNT[:, mb, bs], PT[:, (blk % 2) * 256 + mb * 128:(blk % 2) * 256 + (mb + 1) * 128],
                               ["PT"], ["PNT"])
                    for blk in range(0, TT // 128, 2):
                        la = S.record(lambda: attn_blk(blk))
                        lb_ = S.record(lambda: attn_blk(blk + 1))
                        S.play(la, lb_)
                    for dch in range(DHC):
                        j = hd * DHC + dch
                        P = PB[dch % 2]; Pn = "PB%d" % (dch % 2)
                        mmg(P[:, :], [(VT(mb, j), PNT[:, mb, :]) for mb in range(NM // 128)], Nall + ["PNT"], [Pn])
                        cp("act", H_(j), P[:, :], [Pn], [Hn(j)])
                wsq = WStream([("wo", j, 0) for j in range(KC)])
                for i in range(KC):
                    s_ = wsq.next()
                    P = PB[i % 2]; Pn = "PB%d" % (i % 2)
                    mmg(P[:, :], [(wl(s_, k), H_(k)) for k in range(KC)], ["WS%d" % s_] + [Hn(k) for k in range(KC)], [Pn])
                    tt("dve", X(i), P[:, :], X(i), ALU.add, [Pn, Xn(i)], [Xn(i)])
                    sumsq_push(i, KC, TT, X, Xn)
                sumsq_flush()
                ffn("f2gu", "f2d", c.o_n["n2"], presummed=True)
                SUM = PB[6]
                act(RS[:, :], SUM[:, :], AF.Sqrt, ["PB6"], ["RS"], bias=float(D * EPS), scale=1.0)
                S.op("dve", lambda e, ctx: e.reciprocal(RS[:, :], RS[:, :]), r=["RS"], w=["RS"])
                for i in range(KC):
                    w_ = NWS[:, c.o_n["nfin"] + i:c.o_n["nfin"] + i + 1]
                    stt("dve", X(i), X(i), w_, RS[:, :], ALU.mult, ALU.mult, [Xn(i), "NWS", "RS"], [Xn(i)])
                dma("sp", out_v[:, :, t0:t0 + TT], XR[:, 0:KC, :], [Xn(i) for i in range(KC)], ["out"], "outst")

        S.op("sp", lambda e, ctx: None, r=["out", "dbg"], w=[])
        S.finalize(nc, None)
        sems = {}
        for k in S.maxcnt:
            sems[k] = es.enter_context(nc.semaphore("s_%s_%s" % k))
        with nc.Block() as block:
            S.emit(nc, block, sems)
    return nc, S


def _tiles(W, D):
    K, N = W.shape
    KC = D // 128
    a = W.reshape(K // D, KC, 128, N // 128, 128)
    a = a.transpose(0, 3, 2, 1, 4)
    return np.ascontiguousarray(a).reshape(K // D, N // 128, 128, D)


def _fm(v, KC):
    return np.ascontiguousarray(v.reshape(KC, 128).T)


def make_inputs(cfg, x, mem, ffn1_norm, ffn1_w_gate, ffn1_w_up, ffn1_w_down, mix_norm, w_in, conv_w, conv_b,
                lru_w_a, lru_b_a, lru_w_x, lru_b_x, lru_lambda, hg_lb_logits, hg_gnorm, w_out,
                xattn_norm, mem_norm, xattn_w_q, xattn_w_k, xattn_w_v, xattn_w_o,
                ffn2_norm, ffn2_w_gate, ffn2_w_up, ffn2_w_down, final_norm):
    c = cfg
    D, KC, NF, NS, T, LC, HC, YC, LW, HW = c.D, c.KC, c.NF, c.NS, c.T, c.LC, c.HC, c.YC, c.LW, c.HW
    f32 = np.float32
    A = lambda a: np.asarray(a, dtype=f32)
    groups = {}

    def gu(wgt, wup):
        tg = _tiles(A(wgt)[0], D)[0]
        tu = _tiles(A(wup)[0], D)[0]
        out = np.empty((2 * NF, 128, D), f32)
        out[0::2] = tg; out[1::2] = tu
        return out
    groups["f1gu"] = gu(ffn1_w_gate, ffn1_w_up)
    groups["f1d"] = _tiles(A(ffn1_w_down)[0], D).reshape(NS * KC, 128, D)
    groups["f2gu"] = gu(ffn2_w_gate, ffn2_w_up)
    groups["f2d"] = _tiles(A(ffn2_w_down)[0], D).reshape(NS * KC, 128, D)
    wi = _tiles(A(w_in)[0], D)[0]
    order = []
    for ch in range(LC):
        order.append(ch)
        order.append(LW // 128 + ch)
    for hd in range(HC):
        for sec in range(4):
            order.append(2 * LW // 128 + sec * (HW // 128) + hd)
    groups["win"] = wi[order]
    groups["wout"] = _tiles(A(w_out)[0], D)[0]
    groups["wq"] = _tiles(A(xattn_w_q)[0], D)[0]
    groups["wk"] = _tiles(A(xattn_w_k)[0], D)[0]
    groups["wv"] = _tiles(A(xattn_w_v)[0], D)[0]
    groups["wo"] = _tiles(A(xattn_w_o)[0], D)[0]
    wsh = np.concatenate([groups[gn] for gn, cnt in c.groups], axis=0).reshape(c.ntile8 * 128, D)
    cf = np.zeros((128, c.ncf), f32)
    cf[:, c.o_id:c.o_id + 128] = np.eye(128, dtype=f32)
    s_ = np.arange(128)[:, None]; t_ = np.arange(128)[None, :]
    cf[:, c.o_mk:c.o_mk + 128] = ((s_ // 64 == t_ // 64) & (s_ <= t_)).astype(f32)
    cm = np.ones(TT, f32); cm[0::64] = 0.0
    cf[:, c.o_cm:c.o_cm + TT] = cm[None, :]
    x = A(x); mem = A(mem)
    cw_ = A(conv_w)[0][:, 0, :]
    sm = np.zeros((128, c.nsm), f32)
    for nm, v in (("n1", ffn1_norm), ("nmix", mix_norm), ("nx", xattn_norm), ("nmem", mem_norm), ("n2", ffn2_norm)):
        sm[:, c.o_n[nm]:c.o_n[nm] + KC] = _fm(A(v)[0], KC)
    sm[:, c.o_n["nfin"]:c.o_n["nfin"] + KC] = _fm(A(final_norm), KC)
    for ch in range(LC):
        sl = slice(ch * 128, (ch + 1) * 128)
        for j in range(4):
            sm[:, c.o_cw + ch * 4 + j] = cw_[j, sl]
        sm[:, c.o_cb + ch] = A(conv_b)[0][sl]
        sm[:, c.o_ba + ch] = A(lru_b_a)[0][ch]
        sm[:, c.o_bx + ch] = A(lru_b_x)[0][ch]
        sm[:, c.o_lam + ch] = A(lru_lambda)[0][sl]
    for hd in range(HC):
        sm[:, c.o_lb0 + hd] = A(hg_lb_logits)[0][hd * 128:(hd + 1) * 128]
        sm[:, c.o_lb1 + hd] = A(hg_lb_logits)[1][hd * 128:(hd + 1) * 128]
    sm[:, c.o_gn] = A(hg_gnorm)[0]
    gw = np.empty((128, 2, LC, 128), f32)
    for ch in range(LC):
        gw[:, 0, ch, :] = A(lru_w_a)[0][ch]
        gw[:, 1, ch, :] = A(lru_w_x)[0][ch]
    gw = gw.reshape(128, 2 * LC * 128)
    in_maps = []
    if not c.split:
        for core in range(x.shape[0]):
            in_maps.append({
                "xT": np.ascontiguousarray(x[core].T),
                "memT": np.ascontiguousarray(mem[core].T),
                "wsh": wsh, "smd": sm, "cfd": cf, "gwd": gw,
            })
        return in_maps
    for core in range(2 * x.shape[0]):
        b = core // 2; g = core % 2
        smc = sm.copy()
        smc[:, c.o_flag] = float(g)
        in_maps.append({
            "xT": np.ascontiguousarray(x[b, g * T:(g + 1) * T, :].T),
            "xP": np.ascontiguousarray(x[b, 0:T, :].T),
            "memT": np.ascontiguousarray(mem[b].T),
            "wsh": wsh, "smd": smc, "cfd": cf, "gwd": gw,
        })
    return in_maps


def kernel(**inputs):
    x = inputs["x"]
    B, S_, D = x.shape
    DFF = inputs["ffn1_w_gate"].shape[-1]
    cfg = Cfg(D, DFF, S_, inputs["mem"].shape[1], split=True)
    in_maps = make_inputs(cfg, **inputs)
    nc, _ = build(cfg)
    ncore = len(in_maps)
    res = run_bass_kernel_spmd(nc, in_maps, core_ids=list(range(ncore)))
    out = np.empty((B, S_, D), np.float32)
    T = cfg.T
    for core in range(ncore):
        b = core // 2; g = core % 2
        out[b, g * T:(g + 1) * T, :] = res.results[core]["outT"].T
    return out
```

```python
import os
import numpy as np
import concourse.bass as bass
import concourse.mybir as mybir
from concourse.bass_utils import run_bass_kernel_spmd

F32 = mybir.dt.float32
BF16 = mybir.dt.bfloat16
AF = mybir.ActivationFunctionType
ALU = mybir.AluOpType
EPS = 1e-6
SAME_ENGINE_SYNC = True
NOSYNC_ENGS = set(os.environ.get('NOSYNC', '').split(',')) - {''}
NW = 4
TT = 512


class Cfg:
    def __init__(self, D, DFF, S, NM=256, split=True):
        self.split = split
        self.D = D; self.KC = D // 128; self.DFF = DFF; self.NF = DFF // 128; self.NS = DFF // D
        self.S = S; self.T = S // 2 if split else S; self.NTO = self.T // TT; self.NTF = self.NTO
        self.NTP = self.NTO if split else 0
        self.LW = D // 2; self.HW = D // 2
        self.LCH = self.LW // 128; self.LC = self.LCH
        self.HH = self.HW // 128; self.HC = self.HH
        self.YC = self.LC + self.HC
        self.NM = NM; self.XH = 4; self.DH = D // 4; self.DHC = self.DH // 128
        self.WIN_HALF = 2 * self.LC + 4 * self.HC
        self.groups = [("f1gu", 2 * self.NF), ("f1d", self.NS * self.KC), ("win", self.WIN_HALF),
                       ("wout", self.KC), ("wq", self.KC), ("wk", self.KC), ("wv", self.KC), ("wo", self.KC),
                       ("f2gu", 2 * self.NF), ("f2d", self.NS * self.KC)]
        self.ntile8 = sum(c for _, c in self.groups)
        KC, LC, HC = self.KC, self.LC, self.HC
        o = 0
        self.o_n = {}
        for nm in ("n1", "nmix", "nx", "nmem", "n2", "nfin"):
            self.o_n[nm] = o; o += KC
        self.o_cw = o; o += LC * 4
        self.o_cb = o; o += LC
        self.o_ba = o; o += LC
        self.o_bx = o; o += LC
        self.o_lam = o; o += LC
        self.o_lb0 = o; o += HC
        self.o_lb1 = o; o += HC
        self.o_gn = o; o += 1
        self.o_flag = o; o += 1
        self.nsm = o
        self.o_id = 0; self.o_mk = 128; self.o_cm = 256; self.ncf = 256 + TT


class Op:
    __slots__ = ("idx", "eng", "fn", "deps", "dma", "signal", "val", "waits", "amt")

    def __init__(self, idx, eng, fn, deps, dma, amt):
        self.idx = idx; self.eng = eng; self.fn = fn; self.deps = deps; self.dma = dma
        self.signal = False; self.val = 0; self.waits = []; self.amt = amt


class Sched:
    ENGS = ("pe", "act", "dve", "pool", "sp")

    def __init__(self):
        self.ops = []
        self.lastw = {}
        self.readers = {}
        self.psum = set()
        self.rec = None

    def record(self, f):
        old = self.rec
        self.rec = []
        f()
        out = self.rec
        self.rec = old
        return out

    def play(self, *lists):
        lists = [l for l in lists if l]
        pos = [0] * len(lists)
        total = sum(len(l) for l in lists)
        for _ in range(total):
            k = min((i for i in range(len(lists)) if pos[i] < len(lists[i])), key=lambda i: pos[i] / len(lists[i]))
            self.op(*lists[k][pos[k]][:4], dma=lists[k][pos[k]][4], amt=lists[k][pos[k]][5])
            pos[k] += 1

    def op(self, eng, fn, r=(), w=(), dma=None, amt=16):
        if self.rec is not None:
            self.rec.append((eng, fn, tuple(r), tuple(w), dma, amt))
            return None
        deps = set()
        for b in r:
            x = self.lastw.get(b)
            if x is not None:
                deps.add(x)
            if b in self.psum:
                deps.update(y for y in self.readers.get(b, ()) if self.ops[y].eng != eng)
        for b in w:
            x = self.lastw.get(b)
            if x is not None:
                deps.add(x)
            deps.update(self.readers.get(b, ()))
        idx = len(self.ops)
        o = Op(idx, eng, fn, deps, dma, amt)
        self.ops.append(o)
        for b in r:
            self.readers.setdefault(b, []).append(idx)
        for b in w:
            self.lastw[b] = idx
            self.readers[b] = []
        return o

    def finalize(self, nc, sems):
        ops = self.ops
        for o in ops:
            best = {}
            for d in o.deps:
                dop = ops[d]
                if dop.dma is not None:
                    key = ("dma", dop.dma)
                else:
                    if dop.eng == o.eng and o.dma is None and (o.eng == "pe" or not SAME_ENGINE_SYNC or o.eng in NOSYNC_ENGS):
                        continue
                    key = ("eng", dop.eng)
                if key not in best or best[key] < d:
                    best[key] = d
            o.deps = best
            for d in best.values():
                ops[d].signal = True
        cnt = {}
        for o in ops:
            if o.dma is not None:
                k = ("dma", o.dma)
                cnt[k] = cnt.get(k, 0) + o.amt
                o.val = cnt[k]
                o.signal = True
            elif o.signal:
                k = ("eng", o.eng)
                cnt[k] = cnt.get(k, 0) + 1
                o.val = cnt[k]
        for o in ops:
            o.waits = [(k, ops[d].val) for k, d in o.deps.items()]
        self.maxcnt = cnt

    def emit(self, nc, block, sems):
        decos = {"pe": block.tensor, "act": block.scalar, "dve": block.vector, "pool": block.gpsimd, "sp": block.sync}
        for en in self.ENGS:
            eops = [o for o in self.ops if o.eng == en]

            def body(e, eops=eops):
                waited = {}
                ctx = {}
                for o in eops:
                    for k, v in o.waits:
                        if waited.get(k, 0) < v:
                            e.wait_ge(sems[k], v)
                            waited[k] = v
                    ins = o.fn(e, ctx)
                    if o.signal:
                        assert ins is not None
                        if o.dma is not None:
                            ins.then_inc(sems[("dma", o.dma)], o.amt)
                        else:
                            ins.then_inc(sems[("eng", o.eng)], 1)
            decos[en](body)


def build(cfg, dbg=None):
    c = cfg
    D, KC, NF, NS, T, NTO, NTF, LC, HC, YC, NM = c.D, c.KC, c.NF, c.NS, c.T, c.NTO, c.NTF, c.LC, c.HC, c.YC, c.NM
    nc = bass.Bass("TRN2", target_bir_lowering=False)
    xT = nc.dram_tensor("xT", [D, T], F32, kind="ExternalInput").ap()
    memT = nc.dram_tensor("memT", [D, NM], F32, kind="ExternalInput").ap()
    xP = nc.dram_tensor("xP", [D, T], F32, kind="ExternalInput").ap() if c.split else None
    wsh = nc.dram_tensor("wsh", [c.ntile8 * 128, D], F32, kind="ExternalInput").ap()
    smd = nc.dram_tensor("smd", [128, c.nsm], F32, kind="ExternalInput").ap()
    cfd = nc.dram_tensor("cfd", [128, c.ncf], F32, kind="ExternalInput").ap()
    gwd = nc.dram_tensor("gwd", [128, 2 * LC * 128], F32, kind="ExternalInput").ap()
    outT = nc.dram_tensor("outT", [D, T], F32, kind="ExternalOutput").ap()
    dbgo = None
    if dbg:
        dbgo = nc.dram_tensor("dbgo", [D, T], F32, kind="ExternalOutput").ap()
    wg = {}
    for gn, cnt in c.groups:
        wg[gn] = nc.dram_tensor("wg_" + gn, [cnt * 128, D], BF16)
    hs = nc.dram_tensor("hs", [D, T], F32)
    xn2i = nc.dram_tensor("xn2i", [D, T], BF16)
    yi = nc.dram_tensor("yi", [YC * 128, T], BF16)
    kvs = nc.dram_tensor("kvs", [128, 2 * KC * NM], BF16)

    S = Sched()
    S.psum = {"PB%d" % i for i in range(7)} | {"PT"}
    NXR = max(KC, 32)
    NHD = max(KC, 32)
    import contextlib
    es = contextlib.ExitStack()
    with es:
        def sb(name, shape, dt):
            return es.enter_context(nc.sbuf_tensor(name, shape, dt))

        def ps(name, shape, dt):
            return es.enter_context(nc.psum_tensor(name, shape, dt))
        XR = sb("XR", [128, NXR, TT], F32)
        XN = sb("XN", [128, KC, TT], BF16)
        HD = sb("HD", [128, NHD, TT], BF16)
        WS = sb("WS", [128, NW, D], BF16)
        TMP = sb("TMP", [128, 4, TT], F32)
        TMPB = sb("TMPB", [128, 4, TT], BF16)
        SM = sb("SM", [128, c.nsm], F32)
        CF = sb("CF", [128, c.ncf], F32)
        GWB = sb("GWB", [128, 2 * LC * 128], BF16)
        IDB = sb("IDB", [128, 128], BF16)
        MKB = sb("MKB", [128, 128], BF16)
        ONESB = sb("ONESB", [128, 128], BF16)
        RS = sb("RS", [128, TT], F32)
        NWS = sb("NWS", [128, 6 * KC], F32)
        SLAM = sb("SLAM", [128, LC], F32)
        LB = sb("LB", [128, 2 * HC], F32)
        GNW = sb("GNW", [128, 1], F32)
        LXW = sb("LXW", [128, 2, TT + 3], F32)
        HALO = sb("HALO", [128, LC, 3], F32)
        HCAR = sb("HCAR", [128, LC], F32)
        SST = sb("SST", [128, HC, 128], F32)
        SSB = sb("SSB", [128, HC, 128], BF16)
        EBL2 = sb("EBL", [128, 2, 8], F32)
        SMX2 = sb("SMX", [128, 16], F32)
        if HC * 128 >= 2 * TT + 2 * NM:
            SSBf = SSB[:, :, :].rearrange("p a b -> p (a b)")
            PNT = SSBf[:, 0:2 * TT].rearrange("p (a b) -> p a b", a=2)
            PRB = SSBf[:, 2 * TT:2 * TT + 2 * NM].rearrange("p (a b) -> p a b", a=2)
        else:
            PNT = sb("PNT", [128, 2, TT], BF16)
            PRB = sb("PRB", [128, 2, NM], BF16)
        PB = [ps("PB%d" % i, [128, TT], F32) for i in range(7)]
        PT = ps("PT", [128, 1024], BF16)

        cnt_names = {}

        def X(i):
            return XR[:, i, :]

        def N_(i):
            return XN[:, i, :]

        def H_(i):
            return HD[:, i, :]

        def dma(eng, out, in_, r, w, key):
            S.op(eng, lambda e, ctx, out=out, in_=in_: e.dma_start(out=out, in_=in_), r=r, w=w, dma=key)

        def act(out, in_, func, r, w, bias=None, scale=None, accum=None):
            kw = {}
            if bias is not None:
                kw["bias"] = bias
            if scale is not None:
                kw["scale"] = scale
            if accum is not None:
                kw["accum_out"] = accum
            S.op("act", lambda e, ctx: e.activation(out, in_, func, **kw), r=r, w=w)

        def tt(eng, out, a, b, op, r, w):
            S.op(eng, lambda e, ctx: e.tensor_tensor(out, a, b, op), r=r, w=w)

        def ts(eng, out, a, s1, s2, op0, op1, r, w):
            if op1 is None:
                S.op(eng, lambda e, ctx: e.tensor_scalar(out, a, s1, None, op0), r=r, w=w)
            else:
                S.op(eng, lambda e, ctx: e.tensor_scalar(out, a, s1, s2, op0, op1), r=r, w=w)

        def stt(eng, out, a, s, b, op0, op1, r, w):
            S.op(eng, lambda e, ctx: e.scalar_tensor_tensor(out, a, s, b, op0, op1), r=r, w=w)

        def cp(eng, out, in_, r, w):
            if eng == "act":
                S.op("act", lambda e, ctx: e.copy(out, in_), r=r, w=w)
            else:
                S.op(eng, lambda e, ctx: e.tensor_copy(out, in_), r=r, w=w)

        def mmg(out, pairs, r, w):
            n = len(pairs)

            def fn(e, ctx):
                ins = None
                for i, (l, rr) in enumerate(pairs):
                    ins = e.matmul(out, l, rr, start=(i == 0), stop=(i == n - 1))
                return ins
            S.op("pe", fn, r=r, w=w)

        def mm1(out, l, rr, start, stop, r, w):
            S.op("pe", lambda e, ctx: e.matmul(out, l, rr, start=start, stop=stop, skip_group_check=True), r=r, w=w)

        def tr(out, in_, r, w):
            S.op("pe", lambda e, ctx: e.transpose(out, in_, IDB[:, :]), r=r + ["IDB"], w=w)

        wstate = {"n": 0, "c": 0}

        class WStream:
            def __init__(self, tiles, ring=None):
                self.tiles = tiles; self.issued = 0; self.cur = 0; self.slots = []; self.ring = ring; self.k = 0

            def _issue(self, k):
                gn, ti, dyn = self.tiles[k]
                if self.ring is None:
                    slot = wstate["n"] % NW
                    wstate["n"] += 1
                else:
                    slot = self.ring[self.k % len(self.ring)]
                    self.k += 1
                self.slots.append(slot)
                dst = WS[:, slot, :]
                src = wg[gn][ti * 128:(ti + 1) * 128, :]
                dma("sp", dst, src, ["wg_%s_%d" % (gn, ti // 16)], ["WS%d" % slot], "WS%d" % slot)

            def next(self):
                depth = NW if self.ring is None else len(self.ring) + 1
                while self.issued < len(self.tiles) and self.issued < self.cur + depth - 1:
                    self._issue(self.issued); self.issued += 1
                if self.issued <= self.cur:
                    self._issue(self.issued); self.issued += 1
                slot = self.slots[self.cur]
                self.cur += 1
                return slot

        def wl(slot, kc):
            return WS[:, slot, kc * 128:(kc + 1) * 128]

        pbrot = {"i": 0}

        dma("sp", SM[:, :], smd[:, :], [], ["SM"], "SM")
        dma("sp", CF[:, :], cfd[:, :], [], ["CF"], "CF")
        dma("pool", GWB[:, :], gwd[:, :], [], ["GWB"], "GWB")
        cp("dve", IDB[:, :], CF[:, c.o_id:c.o_id + 128], ["CF"], ["IDB"])
        cp("dve", MKB[:, :], CF[:, c.o_mk:c.o_mk + 128], ["CF"], ["MKB"])
        S.op("dve", lambda e, ctx: e.memset(ONESB[:, :], 1.0), w=["ONESB"])
        S.op("dve", lambda e, ctx: e.memset(HALO[:, :, :], 0.0), w=["HALO%d" % i for i in range(LC)])
        S.op("dve", lambda e, ctx: e.memset(HCAR[:, :], 0.0), w=["HCAR%d" % i for i in range(LC)])
        S.op("dve", lambda e, ctx: e.memset(SST[:, :, :], 0.0), w=["SST%d" % i for i in range(HC)])
        S.op("dve", lambda e, ctx: e.memset(SSB[:, :, :], 0.0), w=["SSB%d" % i for i in range(HC)])
        ts("dve", NWS[:, :], SM[:, 0:6 * KC], float(np.sqrt(D)), None, ALU.mult, None, ["SM"], ["NWS"])
        act(SLAM[:, :], SM[:, c.o_lam:c.o_lam + LC], AF.Exp, ["SM"], ["SLAM"], scale=-1.0)
        act(SLAM[:, :], SLAM[:, :], AF.Ln, ["SLAM"], ["SLAM"], bias=1.0)
        ts("dve", SLAM[:, :], SLAM[:, :], -8.0, None, ALU.mult, None, ["SLAM"], ["SLAM"])
        tt("dve", LB[:, 0:HC], SM[:, c.o_lb0:c.o_lb0 + HC], SM[:, c.o_lb1:c.o_lb1 + HC], ALU.subtract, ["SM"], ["LB"])
        act(LB[:, 0:HC], LB[:, 0:HC], AF.Sigmoid, ["LB"], ["LB"])
        ts("dve", LB[:, HC:2 * HC], LB[:, 0:HC], -1.0, 1.0, ALU.mult, ALU.add, ["LB"], ["LB"])
        ts("dve", GNW[:, :], SM[:, c.o_gn:c.o_gn + 1], float(np.sqrt(128.0)), None, ALU.mult, None, ["SM"], ["GNW"])

        CH = 16
        grow = {}
        row = 0
        for gn, cnt in c.groups:
            grow[gn] = row; row += cnt
        cast_plan = []
        gcnt = dict(c.groups)
        f1 = []
        per_slab_gu = 2 * KC
        for s_ in range(NS):
            for k0 in range(s_ * per_slab_gu, (s_ + 1) * per_slab_gu, CH):
                f1.append(("f1gu", k0, min(k0 + CH, (s_ + 1) * per_slab_gu)))
            for k0 in range(s_ * KC, (s_ + 1) * KC, CH):
                f1.append(("f1d", k0, min(k0 + CH, (s_ + 1) * KC)))
        rest = []
        for gn, cnt in c.groups:
            if gn in ("f1gu", "f1d"):
                continue
            for k0 in range(0, cnt, CH):
                rest.append((gn, k0, min(cnt, k0 + CH)))
        cast_state = {"rest": rest}

        def cast_emit(chunks, extra_r=()):
            for gn, k0, k1 in chunks:
                src = wsh[(grow[gn] + k0) * 128:(grow[gn] + k1) * 128, :]
                kk = wstate["c"] % 4
                wstate["c"] += 1
                dma("pool", wg[gn][k0 * 128:k1 * 128, :], src, list(extra_r), ["wg_%s_%d" % (gn, k0 // CH), "wcastbuf%d" % kk],
                    "wcast%d" % kk)

        pace = {"left": max(1, c.NTP + NTO - 2)}

        def pace_step(extra_r):
            nrest = len(cast_state["rest"])
            if nrest:
                cast_some(nrest if pace["left"] <= 1 else -(-nrest // pace["left"]), extra_r)
            pace["left"] = max(1, pace["left"] - 1)

        def cast_some(n, extra_r=()):
            r_ = cast_state["rest"]
            cast_emit(r_[:n], extra_r)
            cast_state["rest"] = r_[n:]
        first_src = (xP if c.NTP else xT).rearrange("(kc p) t -> p kc t", p=128)[:, :, 0:TT]
        preload = dbg is None
        if preload:
            dma("sp", XR[:, 0:KC, :], first_src, [], ["X%d" % i for i in range(KC)], "XR")
        if dbg != "w0":
            cast_emit(f1[0:1], ["X0"] if preload else [])
            cast_emit(f1[1:])
            if c.NTP:
                cast_emit([x_ for x_ in cast_state["rest"] if x_[0] == "win"])
                cast_state["rest"] = [x_ for x_ in cast_state["rest"] if x_[0] != "win"]

        sqs = {"i": 0}

        def sumsq_chunk(i, nchunks, width, src, srcn):
            SUM = PB[6]
            k = sqs["i"]; sqs["i"] += 1
            t_ = TMP[:, 2 + k % 2, 0:width]; tn = "TMP%d" % (2 + k % 2)
            act(t_, src(i), AF.Square, [srcn(i)], [tn])
            hi = TMPB[:, (2 * k) % 4, 0:width]; hin = "TMPB%d" % ((2 * k) % 4)
            lo = TMPB[:, (2 * k + 1) % 4, 0:width]; lon = "TMPB%d" % ((2 * k + 1) % 4)
            cp("dve", hi, t_, [tn], [hin])
            tt("dve", lo, t_, hi, ALU.subtract, [tn, hin], [lon])
            mm1(SUM[:, 0:width], ONESB[:, :], hi, i == 0, False, ["ONESB", hin], ["PB6"])
            mm1(SUM[:, 0:width], ONESB[:, :], lo, False, i == nchunks - 1, ["ONESB", lon], ["PB6"])

        def sumsq(nchunks, width, src, srcn):
            for i in range(nchunks):
                sumsq_chunk(i, nchunks, width, src, srcn)

        sq_pend = []
        SQ_LAG = 2

        def sumsq_push(i, nchunks, width, src, srcn):
            sq_pend.append((i, nchunks, width, src, srcn))
            if len(sq_pend) > SQ_LAG:
                sumsq_chunk(*sq_pend.pop(0))

        def sumsq_flush():
            while sq_pend:
                sumsq_chunk(*sq_pend.pop(0))

        def rmsnorm(nchunks, width, src, srcn, wcol, dst, dstn, dmodel, presummed=False):
            SUM = PB[6]
            if not presummed:
                sumsq(nchunks, width, src, srcn)
            act(RS[:, 0:width], SUM[:, 0:width], AF.Sqrt, ["PB6"], ["RS"], bias=float(dmodel * EPS), scale=1.0)
            S.op("dve", lambda e, ctx: e.reciprocal(RS[:, 0:width], RS[:, 0:width]), r=["RS"], w=["RS"])
            for i in range(nchunks):
                stt("dve", dst(i), src(i), NWS[:, wcol + i:wcol + i + 1], RS[:, 0:width], ALU.mult, ALU.mult,
                    [srcn(i), "NWS", "RS"], [dstn(i)])

        Xn = lambda i: "X%d" % i
        Nn = lambda i: "N%d" % i
        Hn = lambda i: "H%d" % i

        def ffn(gu, dn, wcol, presummed=False):
            rmsnorm(KC, TT, X, Xn, wcol, N_, Nn, D, presummed)
            tiles = []
            for s_ in range(NS):
                for j in range(KC):
                    jj = s_ * KC + j
                    tiles.append((gu, 2 * jj, 0)); tiles.append((gu, 2 * jj + 1, 0))
                for i in range(KC):
                    tiles.append((dn, s_ * KC + i, 0))
            wsq = WStream(tiles)
            for s_ in range(NS):
                for j in range(KC):
                    sg = wsq.next(); su = wsq.next()
                    G = PB[(2 * j) % 4]; U = PB[(2 * j + 1) % 4]
                    Gn = "PB%d" % ((2 * j) % 4); Un = "PB%d" % ((2 * j + 1) % 4)
                    mmg(G[:, :], [(wl(sg, k), N_(k)) for k in range(KC)], ["WS%d" % sg] + [Nn(k) for k in range(KC)], [Gn])
                    mmg(U[:, :], [(wl(su, k), N_(k)) for k in range(KC)], ["WS%d" % su] + [Nn(k) for k in range(KC)], [Un])
                    t_ = TMP[:, j % 2, :]
                    act(t_, G[:, :], AF.Silu, [Gn], ["TMP%d" % (j % 2)])
                    tt("dve", H_(j), t_, U[:, :], ALU.mult, ["TMP%d" % (j % 2), Un], [Hn(j)])
                for i in range(KC):
                    sd = wsq.next()
                    Dp = PB[4 + (i % 2)]; Dn = "PB%d" % (4 + (i % 2))
                    mmg(Dp[:, :], [(wl(sd, k), H_(k)) for k in range(KC)], ["WS%d" % sd] + [Hn(k) for k in range(KC)], [Dn])
                    stt("dve", X(i), Dp[:, :], 0.5, X(i), ALU.mult, ALU.add, [Dn, Xn(i)], [Xn(i)])
                    if s_ == NS - 1:
                        sumsq_push(i, KC, TT, X, Xn)
            sumsq_flush()

        xT_v = xT.rearrange("(kc p) t -> p kc t", p=128)
        hs_v = hs.ap().rearrange("(kc p) t -> p kc t", p=128)
        out_v = outT.rearrange("(kc p) t -> p kc t", p=128)
        xn2i_v = xn2i.ap().rearrange("(kc p) t -> p kc t", p=128)
        yi_v = yi.ap().rearrange("(q p) t -> p q t", p=128)
        dbg_v = dbgo.rearrange("(kc p) t -> p kc t", p=128) if dbg else None

        if True:
            do_l = dbg != "p2h"; do_h = dbg != "p2l"
            CM = CF[:, c.o_cm:c.o_cm + TT]
            wtp = {"i": 0}

            def WT():
                i = wtp["i"] % NXR; wtp["i"] += 1
                return XR[:, i, :], "X%d" % i
            wbp = {"i": 0}

            def WB():
                i = wbp["i"] % NHD; wbp["i"] += 1
                return HD[:, i, :], "H%d" % i

            def ystore(yt, ytn, q, tf):
                dma("sp", yi_v[:, q, tf * TT:(tf + 1) * TT], yt, [ytn], ["yi%d" % tf, "ystchain"], "yst")
            prj = {"i": 0}

            def PJ(p=0):
                if prj.get("mode") == "lru":
                    k = prj.get(p, 0); prj[p] = k + 1
                    i = 2 * p + k % 2
                else:
                    i = p
                return PB[i], "PB%d" % i

            def mixer_tile(tf, state_only):
                if not state_only:
                    dma("sp", XN[:, :, :], xn2i_v[:, :, tf * TT:(tf + 1) * TT], ["xn2i%d" % tf], [Nn(i) for i in range(KC)], "XN")

                def want(j):
                    if j < 2 * LC:
                        return do_l and (not state_only or j % 2 == 0)
                    return do_h and (not state_only or (j - 2 * LC) % 4 in (1, 2))
                def stream_of(j):
                    return (j // 2) % 2 if j < 2 * LC else ((j - 2 * LC) // 4) % 2
                wsqs = [WStream([("win", j, 0) for j in range(c.WIN_HALF) if want(j) and stream_of(j) == p_],
                                ring=[2 * p_, 2 * p_ + 1]) for p_ in range(2)]
                Nall = [Nn(k) for k in range(KC)]

                def proj(p=0):
                    s_ = wsqs[p].next()
                    P, Pn = PJ(p)
                    mmg(P[:, :], [(wl(s_, k), N_(k)) for k in range(KC)], ["WS%d" % s_] + Nall, [Pn])
                    return P, Pn
                def lru_chunk(ch):
                    sp_ = ch % 2
                    P, Pn = proj(sp_)
                    Ln = "LXW%d" % (ch % 2)
                    LX = LXW[:, ch % 2, :]
                    cp("dve", LX[:, 0:3], HALO[:, ch, :], ["HALO%d" % ch], [Ln])
                    cp("act", LX[:, 3:TT + 3], P[:, :], [Pn], [Ln])
                    cp("dve", HALO[:, ch, :], LX[:, TT:TT + 3], [Ln], ["HALO%d" % ch])
                    xc, xcn = WT()
                    cw = lambda j, ch=ch: SM[:, c.o_cw + ch * 4 + j:c.o_cw + ch * 4 + j + 1]
                    ts("dve", xc, LX[:, 0:TT], cw(0), SM[:, c.o_cb + ch:c.o_cb + ch + 1], ALU.mult, ALU.add, [Ln, "SM"], [xcn])
                    for j in (1, 2, 3):
                        stt("dve", xc, LX[:, j:j + TT], cw(j), xc, ALU.mult, ALU.add, [Ln, "SM", xcn], [xcn])
                    xcb, xcbn = WB()
                    cp("pool", xcb, xc, [xcn], [xcbn])
                    Pr, Prn = PJ(sp_)
                    mmg(Pr[:, :], [(GWB[:, (0 * LC + ch) * 128:(0 * LC + ch + 1) * 128], xcb)], ["GWB", xcbn], [Prn])
                    Pi, Pin = PJ(sp_)
                    mmg(Pi[:, :], [(GWB[:, (1 * LC + ch) * 128:(1 * LC + ch + 1) * 128], xcb)], ["GWB", xcbn], [Pin])
                    rr, rrn = WT()
                    act(rr, Pr[:, :], AF.Sigmoid, [Prn, "SM"], [rrn], bias=SM[:, c.o_ba + ch:c.o_ba + ch + 1])
                    ig, ign = WT()
                    act(ig, Pi[:, :], AF.Sigmoid, [Pin, "SM"], [ign], bias=SM[:, c.o_bx + ch:c.o_bx + ch + 1])
                    act(rr, rr, AF.Exp, [rrn, "SLAM"], [rrn], scale=SLAM[:, ch:ch + 1])
                    a2, a2n = WT()
                    tt("pool", a2, rr, rr, ALU.mult, [rrn], [a2n])
                    act(a2, a2, AF.Sqrt, [a2n], [a2n], bias=1.0, scale=-1.0)
                    tt("pool", ig, ig, xc, ALU.mult, [ign, xcn], [ign])
                    tt("dve", ig, ig, a2, ALU.mult, [ign, a2n], [ign])
                    hh, hhn = WT()
                    S.op("dve", lambda e, ctx, hh=hh, rr=rr, ig=ig, ch=ch: e.tensor_tensor_scan(
                        hh, rr, ig, HCAR[:, ch:ch + 1], ALU.mult, ALU.add),
                        r=[rrn, ign, "HCAR%d" % ch], w=[hhn])
                    cp("dve", HCAR[:, ch:ch + 1], hh[:, TT - 1:TT], [hhn], ["HCAR%d" % ch])
                    if state_only:
                        return
                    P, Pn = proj(sp_)
                    gs, gsn = WT()
                    cp("act", gs, P[:, :], [Pn], [gsn])
                    g2, g2n = WT()
                    tt("pool", g2, gs, gs, ALU.mult, [gsn], [g2n])
                    ts("pool", g2, g2, 0.044715, 1.0, ALU.mult, ALU.add, [g2n], [g2n])
                    tt("pool", g2, g2, gs, ALU.mult, [g2n, gsn], [g2n])
                    act(g2, g2, AF.Sigmoid, [g2n], [g2n], scale=1.5957691216057308)
                    tt("pool", g2, g2, gs, ALU.mult, [g2n, gsn], [g2n])
                    yt, ytn = WB()
                    tt("dve", yt, hh, g2, ALU.mult, [hhn, g2n], [ytn])
                    ystore(yt, ytn, ch, tf)
                prj["mode"] = "lru"
                for ch in range(0, LC if do_l else 0, 2):
                    la = S.record(lambda: lru_chunk(ch))
                    lb_ = S.record(lambda: lru_chunk(ch + 1)) if ch + 1 < LC else []
                    S.play(la, lb_)
                def head(hd):
                    sp_ = hd % 2
                    EBL = EBL2[:, hd % 2, :]; EBn = "EBL%d" % (hd % 2)
                    if not state_only:
                        Pq, Pqn = proj(sp_)
                        q, qn = WT()
                        act(q, Pq[:, :], AF.Silu, [Pqn], [qn])
                    Pf, Pfn = proj(sp_)
                    f, fn_ = WT()
                    act(f, Pf[:, :], AF.Sigmoid, [Pfn], [fn_])
                    ts("dve", f, f, LB[:, HC + hd:HC + hd + 1], LB[:, hd:hd + 1], ALU.mult, ALU.add, [fn_, "LB"], [fn_])
                    lf, lfn = WT()
                    act(lf, f, AF.Ln, [fn_], [lfn])
                    ts("pool", f, f, -1.0, 1.0, ALU.mult, ALU.add, [fn_], [fn_])
                    bc, bcn = WT()
                    S.op("dve", lambda e, ctx, bc=bc, lf=lf: e.tensor_tensor_scan(bc, CM, lf, 0.0, ALU.mult, ALU.add),
                         r=["CF", lfn], w=[bcn])
                    S.op("act", lambda e, ctx, bc=bc: e.activation(
                        EBL[:, 0:TT // 64], bc.rearrange("p (c s) -> p c s", s=64)[:, :, 63], AF.Exp),
                        r=[bcn], w=[EBn])
                    if not state_only:
                        act(lf, bc, AF.Exp, [bcn], [lfn])
                        qd, qdn = WB()
                        tt("dve", qd, q, lf, ALU.mult, [qn, lfn], [qdn])
                    act(bc, bc, AF.Exp, [bcn], [bcn], scale=-1.0)
                    if not state_only:
                        kd, kdn = WB()
                        tt("pool", kd, f, bc, ALU.mult, [fn_, bcn], [kdn])
                    kd2, kd2n = WB()
                    tt("pool", bc, f, bc, ALU.mult, [fn_, bcn], [bcn])
                    NCH = TT // 64
                    tt("dve", kd2.rearrange("p (c s) -> p c s", s=64), bc.rearrange("p (c s) -> p c s", s=64),
                       EBL[:, 0:NCH].unsqueeze(2).broadcast_to([128, NCH, 64]), ALU.mult, [bcn, EBn], [kd2n])
                    Pv, Pvn = proj(sp_)
                    vb, vbn = WB()
                    cp("act", vb, Pv[:, :], [Pvn], [vbn])
                    if not state_only:
                        Pg, Pgn = proj(sp_)
                        gg, ggn = WT()
                        act(gg, Pg[:, :], AF.Silu, [Pgn], [ggn])
                    HCUT = int(os.environ.get("HCUT", "9"))
                    if HCUT < 1:
                        return
                    kT, kTn = WB()
                    vT, vTn = WB()
                    pt0 = sp_ * 512
                    for blk in range(TT // 128):
                        bs = slice(blk * 128, blk * 128 + 128)
                        tr(PT[:, pt0 + blk * 128:pt0 + (blk + 1) * 128], kd2[:, bs], [kd2n], ["PT"])
                    cp("dve", kT, PT[:, pt0:pt0 + 512], ["PT"], [kTn])
                    for blk in range(TT // 128):
                        bs = slice(blk * 128, blk * 128 + 128)
                        tr(PT[:, pt0 + blk * 128:pt0 + (blk + 1) * 128], vb[:, bs], [vbn], ["PT"])
                    cp("act", vT, PT[:, pt0:pt0 + 512], ["PT"], [vTn])
                    PO = PB[2 + hd % 2]; POn = "PB%d" % (2 + hd % 2); SC = PB[4]; DS = PB[5]
                    if HCUT < 2:
                        return
                    for blk in range(TT // 128):
                        bs = slice(blk * 128, blk * 128 + 128)
                        if not state_only:
                            scs = slice(sp_ * 256 + (blk % 2) * 128, sp_ * 256 + (blk % 2) * 128 + 128)
                            mm1(SC[:, scs], kd[:, bs], qd[:, bs], True, True, [kdn, qdn], ["PB4"])
                            scm, scmn = WB()
                            tt("dve", scm[:, 0:128], SC[:, scs], MKB[:, :], ALU.mult, ["PB4", "MKB"], [scmn])
                            mm1(PO[:, bs], vT[:, bs], scm[:, 0:128], True, False, [vTn, scmn], [POn])
                        for hf in range(2 if HCUT >= 3 else 0):
                            cc = blk * 2 + hf
                            cs = slice(cc * 64, cc * 64 + 64)
                            pr = slice(hf * 64, hf * 64 + 64)
                            if not state_only:
                                mm1(PO[:, cs], SSB[:, hd, :], qd[:, cs], False, hf == 1, ["SSB%d" % hd, qdn], [POn])
                            if HCUT < 4:
                                continue
                            dsl = slice(sp_ * 256 + (cc % 2) * 128, sp_ * 256 + (cc % 2) * 128 + 128)
                            mm1(DS[:, dsl], kT[pr, bs], vT[pr, bs], True, True, [kTn, vTn], ["PB5"])
                            stt("dve", SST[:, hd, :], SST[:, hd, :], EBL[:, cc:cc + 1], DS[:, dsl], ALU.mult, ALU.add,
                                ["SST%d" % hd, EBn, "PB5"], ["SST%d" % hd])
                            if not state_only:
                                cp("act", SSB[:, hd, :], SST[:, hd, :], ["SST%d" % hd], ["SSB%d" % hd])
                    if HCUT < 5 or state_only:
                        return
                    osq, osqn = WT()
                    act(osq, PO[:, :], AF.Square, [POn], [osqn])
                    ohi, ohin = WB()
                    cp("pool", ohi, osq, [osqn], [ohin])
                    olo, olon = WB()
                    tt("pool", olo, osq, ohi, ALU.subtract, [osqn, ohin], [olon])
                    mm1(PB[sp_][:, :], ONESB[:, :], ohi, True, False, ["ONESB", ohin], ["PB%d" % sp_])
                    mm1(PB[sp_][:, :], ONESB[:, :], olo, False, True, ["ONESB", olon], ["PB%d" % sp_])
                    act(osq, PB[sp_][:, :], AF.Sqrt, ["PB%d" % sp_], [osqn], bias=float(128 * EPS), scale=1.0)
                    S.op("dve", lambda e, ctx, osq=osq: e.reciprocal(osq, osq), r=[osqn], w=[osqn])
                    tt("dve", osq, PO[:, :], osq, ALU.mult, [POn, osqn], [osqn])
                    yt, ytn = WB()
                    stt("dve", yt, osq, GNW[:, 0:1], gg, ALU.mult, ALU.mult, [osqn, "GNW", ggn], [ytn])
                    ystore(yt, ytn, LC + hd, tf)
                prj["mode"] = "head"
                for hd in range(0, HC if do_h else 0, 2):
                    ha = S.record(lambda: head(hd))
                    hb = S.record(lambda: head(hd + 1)) if hd + 1 < HC else []
                    S.play(ha, hb)

        if dbg not in ("w0", "w1", "p1"):
            for t in range(c.NTP):
                t0 = t * TT
                if not (preload and t == 0):
                    dma("sp", XR[:, 0:KC, :], xP.rearrange("(kc p) t -> p kc t", p=128)[:, :, t0:t0 + TT], [], [Xn(i) for i in range(KC)], "XR")
                ffn("f1gu", "f1d", c.o_n["n1"])
                rmsnorm(KC, TT, X, Xn, c.o_n["nmix"], N_, Nn, D, presummed=True)
                mixer_tile(-1 - t, True)
                pace_step(["SST%d" % (HC - 1)])
            if c.NTP:
                fl = SM[:, c.o_flag:c.o_flag + 1]
                ts("dve", HCAR[:, :], HCAR[:, :], fl, None, ALU.mult, None, ["SM"] + ["HCAR%d" % i for i in range(LC)], ["HCAR%d" % i for i in range(LC)])
                HALOf = HALO[:, :, :].rearrange("p a b -> p (a b)")
                ts("dve", HALOf, HALOf, fl, None, ALU.mult, None, ["SM"] + ["HALO%d" % i for i in range(LC)], ["HALO%d" % i for i in range(LC)])
                SSTf = SST[:, :, :].rearrange("p a b -> p (a b)")
                ts("dve", SSTf, SSTf, fl, None, ALU.mult, None, ["SM"] + ["SST%d" % i for i in range(HC)], ["SST%d" % i for i in range(HC)])
                cp("act", SSB[:, :, :].rearrange("p a b -> p (a b)"), SSTf, ["SST%d" % i for i in range(HC)], ["SSB%d" % i for i in range(HC)])

        if dbg in ("w0", "w1"):
            dma("sp", XR[:, 0:KC, :], xT_v[:, :, 0:TT], [], [Xn(i) for i in range(KC)], "XR")
            rd = [Xn(i) for i in range(KC)] + ["SLAM", "LB", "GNW", "NWS", "IDB", "MKB", "GWB", "ONESB"]
            if dbg == "w1":
                rd += ["wg_%s_%d" % (gn, k // 16) for gn, cnt in c.groups for k in range(0, cnt, 16)]
            dma("sp", dbg_v[:, :, 0:TT], XR[:, 0:KC, :], rd, ["dbg"], "dbgst")
        for t in range(NTO if dbg not in ("w0", "w1") else 0):
            t0 = t * TT
            if not (preload and t == 0 and not c.NTP):
                dma("sp", XR[:, 0:KC, :], xT_v[:, :, t0:t0 + TT], [], [Xn(i) for i in range(KC)], "XR")
            ffn("f1gu", "f1d", c.o_n["n1"])
            dma("sp", hs_v[:, :, t0:t0 + TT], XR[:, 0:KC, :], [Xn(i) for i in range(KC)], ["hs%d" % t], "hsst")
            pace_step(["hs%d" % t])
            rmsnorm(KC, TT, X, Xn, c.o_n["nmix"], N_, Nn, D, presummed=True)
            dma("sp", xn2i_v[:, :, t0:t0 + TT], XN[:, :, :], [Nn(i) for i in range(KC)], ["xn2i%d" % t], "xn2st")
            if dbg == "p1":
                dma("sp", dbg_v[:, :, t0:t0 + TT], XR[:, 0:KC, :], [Xn(i) for i in range(KC)], ["dbg"], "dbgst")


        if dbg not in ("p1", "w0", "w1"):
            for tf in range(NTF):
                mixer_tile(tf, False)
            if dbg in ("p2", "p2l", "p2h"):
                for t in range(NTO):
                    dma("sp", HD[:, 0:YC, :], yi_v[:, :, t * TT:(t + 1) * TT], ["yi%d" % t], [Hn(i) for i in range(YC)], "HDl")
                    for i in range(KC):
                        cp("dve", X(i), H_(i), [Hn(i)], [Xn(i)])
                    dma("sp", dbg_v[:, :, t * TT:(t + 1) * TT], XR[:, 0:KC, :], [Xn(i) for i in range(KC)], ["dbg"], "dbgst")

        if dbg is None or dbg == "p3":
            kvs_v = kvs.ap()
            dma("sp", XR[:, 0:KC, 0:NM], memT.rearrange("(kc p) m -> p kc m", p=128), [], [Xn(i) for i in range(KC)], "XR")
            rmsnorm(KC, NM, lambda i: XR[:, i, 0:NM], Xn, c.o_n["nmem"], lambda i: XN[:, i, 0:NM], Nn, D)
            wsq = WStream([("wk", j, 0) for j in range(KC)] + [("wv", j, 0) for j in range(KC)])
            for which in range(2):
                for j in range(KC):
                    s_ = wsq.next()
                    P = PB[j % 2]; Pn = "PB%d" % (j % 2)
                    mmg(P[:, 0:NM], [(wl(s_, k), XN[:, k, 0:NM]) for k in range(KC)], ["WS%d" % s_] + [Nn(k) for k in range(KC)], [Pn])
                    cp("act", HD[:, j, which * NM:(which + 1) * NM], P[:, 0:NM], [Pn], [Hn(j)])
            XNf = XN[:, :, :].rearrange("p a b -> p (a b)")
            for j in range(KC):
                for mb in range(NM // 128):
                    tr(PT[:, (j % 4) * 256 + mb * 128:(j % 4) * 256 + (mb + 1) * 128],
                       HD[:, j, NM + mb * 128:NM + (mb + 1) * 128], [Hn(j)], ["PT"])
                for mb in range(NM // 128):
                    cp("dve", XNf[:, mb * D + j * 128:mb * D + (j + 1) * 128],
                       PT[:, (j % 4) * 256 + mb * 128:(j % 4) * 256 + (mb + 1) * 128], ["PT"], [Nn(k) for k in range(KC)])
            dma("sp", kvs_v[:, 0:KC * NM].rearrange("p (j m) -> p j m", m=NM), HD[:, 0:KC, 0:NM], [Hn(j) for j in range(KC)], ["kvsK"], "kvst")
            dma("sp", kvs_v[:, KC * NM:2 * KC * NM], XNf[:, 0:2 * D], [Nn(k) for k in range(KC)], ["kvsV"], "kvst2")

            for t in range(NTO):
                t0 = t * TT
                dma("sp", XR[:, 0:KC, :], hs_v[:, :, t0:t0 + TT], ["hs%d" % t], [Xn(i) for i in range(KC)], "XR")

                dma("sp", HD[:, 0:YC, :], yi_v[:, :, t0:t0 + TT], ["yi%d" % t], [Hn(i) for i in range(YC)], "HDl")
                wsq = WStream([("wout", j, 0) for j in range(KC)] + [("wq", j, 0) for j in range(KC)])
                for i in range(KC):
                    s_ = wsq.next()
                    P = PB[i % 2]; Pn = "PB%d" % (i % 2)
                    mmg(P[:, :], [(wl(s_, k), H_(k)) for k in range(KC)], ["WS%d" % s_] + [Hn(k) for k in range(KC)], [Pn])
                    tt("dve", X(i), P[:, :], X(i), ALU.add, [Pn, Xn(i)], [Xn(i)])
                    sumsq_push(i, KC, TT, X, Xn)
                sumsq_flush()
                rmsnorm(KC, TT, X, Xn, c.o_n["nx"], N_, Nn, D, presummed=True)
                for i in range(KC):
                    s_ = wsq.next()
                    P = PB[i % 2]; Pn = "PB%d" % (i % 2)
                    mmg(P[:, :], [(wl(s_, k), N_(k)) for k in range(KC)], ["WS%d" % s_] + [Nn(k) for k in range(KC)], [Pn])
                    cp("act", H_(i), P[:, :], [Pn], [Hn(i)])
                Nall = [Nn(k) for k in range(KC)]
                dma("sp", XNf[:, 0:2 * KC * NM], kvs_v[:, :], ["kvsK", "kvsV"], Nall, "XN")
                KT = lambda j: XNf[:, j * NM:(j + 1) * NM]
                VT = lambda mb, j: XNf[:, KC * NM + mb * D + j * 128:KC * NM + mb * D + (j + 1) * 128]
                XH, DHC = c.XH, c.DHC
                sc_scale = float(c.DH ** -0.5)
                for hd in range(XH):
                    def attn_blk(blk, hd=hd):
                        bs = slice(blk * 128, blk * 128 + 128)
                        Psn = "PB%d" % (2 + blk % 2); Ps = PB[2 + blk % 2]
                        SMX = SMX2[:, 4 * (blk % 2):4 * (blk % 2) + 4]; SMn = "SMX%d" % (blk % 2)
                        mmg(Ps[:, 0:NM], [(HD[:, hd * DHC + dch, bs], KT(hd * DHC + dch)) for dch in range(DHC)],
                            [Hn(hd * DHC + dch) for dch in range(DHC)] + Nall, [Psn])
                        S.op("dve", lambda e, ctx, Ps=Ps: e.reduce_max(SMX[:, 0:1], Ps[:, 0:NM], mybir.AxisListType.X),
                             r=[Psn], w=[SMn])
                        ts("dve", SMX[:, 1:2], SMX[:, 0:1], -sc_scale, None, ALU.mult, None, [SMn], [SMn])
                        ex, exn = TMP[:, blk % 4, 0:NM], "TMP%d" % (blk % 4)
                        act(ex, Ps[:, 0:NM], AF.Exp, [Psn, SMn], [exn, SMn], bias=SMX[:, 1:2], scale=sc_scale, accum=SMX[:, 2:3])
                        S.op("dve", lambda e, ctx: e.reciprocal(SMX[:, 3:4], SMX[:, 2:3]), r=[SMn], w=[SMn])
                        ts("dve", PRB[:, blk % 2, :], ex, SMX[:, 3:4], None, ALU.mult, None, [exn, SMn], ["PRB%d" % (blk % 2)])
                        for mb in range(NM // 128):
                            tr(PT[:, (blk % 2) * 256 + mb * 128:(blk % 2) * 256 + (mb + 1) * 128],
                               PRB[:, blk % 2, mb * 128:(mb + 1) * 128], ["PRB%d" % (blk % 2)], ["PT"])
                        for mb in range(NM // 128):
                            cp("act", PNT[:, mb, bs], PT[:, (blk % 2) * 256 + mb * 128:(blk % 2) * 256 + (mb + 1) * 128],
                               ["PT"], ["PNT"])
                    for blk in range(0, TT // 128, 2):
                        la = S.record(lambda: attn_blk(blk))
                        lb_ = S.record(lambda: attn_blk(blk + 1))
                        S.play(la, lb_)
                    for dch in range(DHC):
                        j = hd * DHC + dch
                        P = PB[dch % 2]; Pn = "PB%d" % (dch % 2)
                        mmg(P[:, :], [(VT(mb, j), PNT[:, mb, :]) for mb in range(NM // 128)], Nall + ["PNT"], [Pn])
                        cp("act", H_(j), P[:, :], [Pn], [Hn(j)])
                wsq = WStream([("wo", j, 0) for j in range(KC)])
                for i in range(KC):
                    s_ = wsq.next()
                    P = PB[i % 2]; Pn = "PB%d" % (i % 2)
                    mmg(P[:, :], [(wl(s_, k), H_(k)) for k in range(KC)], ["WS%d" % s_] + [Hn(k) for k in range(KC)], [Pn])
                    tt("dve", X(i), P[:, :], X(i), ALU.add, [Pn, Xn(i)], [Xn(i)])
                    sumsq_push(i, KC, TT, X, Xn)
                sumsq_flush()
                ffn("f2gu", "f2d", c.o_n["n2"], presummed=True)
                SUM = PB[6]
                act(RS[:, :], SUM[:, :], AF.Sqrt, ["PB6"], ["RS"], bias=float(D * EPS), scale=1.0)
                S.op("dve", lambda e, ctx: e.reciprocal(RS[:, :], RS[:, :]), r=["RS"], w=["RS"])
                for i in range(KC):
                    w_ = NWS[:, c.o_n["nfin"] + i:c.o_n["nfin"] + i + 1]
                    stt("dve", X(i), X(i), w_, RS[:, :], ALU.mult, ALU.mult, [Xn(i), "NWS", "RS"], [Xn(i)])
                dma("sp", out_v[:, :, t0:t0 + TT], XR[:, 0:KC, :], [Xn(i) for i in range(KC)], ["out"], "outst")

        S.op("sp", lambda e, ctx: None, r=["out", "dbg"], w=[])
        S.finalize(nc, None)
        sems = {}
        for k in S.maxcnt:
            sems[k] = es.enter_context(nc.semaphore("s_%s_%s" % k))
        with nc.Block() as block:
            S.emit(nc, block, sems)
    return nc, S


def _tiles(W, D):
    K, N = W.shape
    KC = D // 128
    a = W.reshape(K // D, KC, 128, N // 128, 128)
    a = a.transpose(0, 3, 2, 1, 4)
    return np.ascontiguousarray(a).reshape(K // D, N // 128, 128, D)


def _fm(v, KC):
    return np.ascontiguousarray(v.reshape(KC, 128).T)


def make_inputs(cfg, x, mem, ffn1_norm, ffn1_w_gate, ffn1_w_up, ffn1_w_down, mix_norm, w_in, conv_w, conv_b,
                lru_w_a, lru_b_a, lru_w_x, lru_b_x, lru_lambda, hg_lb_logits, hg_gnorm, w_out,
                xattn_norm, mem_norm, xattn_w_q, xattn_w_k, xattn_w_v, xattn_w_o,
                ffn2_norm, ffn2_w_gate, ffn2_w_up, ffn2_w_down, final_norm):
    c = cfg
    D, KC, NF, NS, T, LC, HC, YC, LW, HW = c.D, c.KC, c.NF, c.NS, c.T, c.LC, c.HC, c.YC, c.LW, c.HW
    f32 = np.float32
    A = lambda a: np.asarray(a, dtype=f32)
    groups = {}

    def gu(wgt, wup):
        tg = _tiles(A(wgt)[0], D)[0]
        tu = _tiles(A(wup)[0], D)[0]
        out = np.empty((2 * NF, 128, D), f32)
        out[0::2] = tg; out[1::2] = tu
        return out
    groups["f1gu"] = gu(ffn1_w_gate, ffn1_w_up)
    groups["f1d"] = _tiles(A(ffn1_w_down)[0], D).reshape(NS * KC, 128, D)
    groups["f2gu"] = gu(ffn2_w_gate, ffn2_w_up)
    groups["f2d"] = _tiles(A(ffn2_w_down)[0], D).reshape(NS * KC, 128, D)
    wi = _tiles(A(w_in)[0], D)[0]
    order = []
    for ch in range(LC):
        order.append(ch)
        order.append(LW // 128 + ch)
    for hd in range(HC):
        for sec in range(4):
            order.append(2 * LW // 128 + sec * (HW // 128) + hd)
    groups["win"] = wi[order]
    groups["wout"] = _tiles(A(w_out)[0], D)[0]
    groups["wq"] = _tiles(A(xattn_w_q)[0], D)[0]
    groups["wk"] = _tiles(A(xattn_w_k)[0], D)[0]
    groups["wv"] = _tiles(A(xattn_w_v)[0], D)[0]
    groups["wo"] = _tiles(A(xattn_w_o)[0], D)[0]
    wsh = np.concatenate([groups[gn] for gn, cnt in c.groups], axis=0).reshape(c.ntile8 * 128, D)
    cf = np.zeros((128, c.ncf), f32)
    cf[:, c.o_id:c.o_id + 128] = np.eye(128, dtype=f32)
    s_ = np.arange(128)[:, None]; t_ = np.arange(128)[None, :]
    cf[:, c.o_mk:c.o_mk + 128] = ((s_ // 64 == t_ // 64) & (s_ <= t_)).astype(f32)
    cm = np.ones(TT, f32); cm[0::64] = 0.0
    cf[:, c.o_cm:c.o_cm + TT] = cm[None, :]
    x = A(x); mem = A(mem)
    cw_ = A(conv_w)[0][:, 0, :]
    sm = np.zeros((128, c.nsm), f32)
    for nm, v in (("n1", ffn1_norm), ("nmix", mix_norm), ("nx", xattn_norm), ("nmem", mem_norm), ("n2", ffn2_norm)):
        sm[:, c.o_n[nm]:c.o_n[nm] + KC] = _fm(A(v)[0], KC)
    sm[:, c.o_n["nfin"]:c.o_n["nfin"] + KC] = _fm(A(final_norm), KC)
    for ch in range(LC):
        sl = slice(ch * 128, (ch + 1) * 128)
        for j in range(4):
            sm[:, c.o_cw + ch * 4 + j] = cw_[j, sl]
        sm[:, c.o_cb + ch] = A(conv_b)[0][sl]
        sm[:, c.o_ba + ch] = A(lru_b_a)[0][ch]
        sm[:, c.o_bx + ch] = A(lru_b_x)[0][ch]
        sm[:, c.o_lam + ch] = A(lru_lambda)[0][sl]
    for hd in range(HC):
        sm[:, c.o_lb0 + hd] = A(hg_lb_logits)[0][hd * 128:(hd + 1) * 128]
        sm[:, c.o_lb1 + hd] = A(hg_lb_logits)[1][hd * 128:(hd + 1) * 128]
    sm[:, c.o_gn] = A(hg_gnorm)[0]
    gw = np.empty((128, 2, LC, 128), f32)
    for ch in range(LC):
        gw[:, 0, ch, :] = A(lru_w_a)[0][ch]
        gw[:, 1, ch, :] = A(lru_w_x)[0][ch]
    gw = gw.reshape(128, 2 * LC * 128)
    in_maps = []
    if not c.split:
        for core in range(x.shape[0]):
            in_maps.append({
                "xT": np.ascontiguousarray(x[core].T),
                "memT": np.ascontiguousarray(mem[core].T),
                "wsh": wsh, "smd": sm, "cfd": cf, "gwd": gw,
            })
        return in_maps
    for core in range(2 * x.shape[0]):
        b = core // 2; g = core % 2
        smc = sm.copy()
        smc[:, c.o_flag] = float(g)
        in_maps.append({
            "xT": np.ascontiguousarray(x[b, g * T:(g + 1) * T, :].T),
            "xP": np.ascontiguousarray(x[b, 0:T, :].T),
            "memT": np.ascontiguousarray(mem[b].T),
            "wsh": wsh, "smd": smc, "cfd": cf, "gwd": gw,
        })
    return in_maps


def kernel(**inputs):
    x = inputs["x"]
    B, S_, D = x.shape
    DFF = inputs["ffn1_w_gate"].shape[-1]
    cfg = Cfg(D, DFF, S_, inputs["mem"].shape[1], split=True)
    in_maps = make_inputs(cfg, **inputs)
    nc, _ = build(cfg)
    ncore = len(in_maps)
    res = run_bass_kernel_spmd(nc, in_maps, core_ids=list(range(ncore)))
    out = np.empty((B, S_, D), np.float32)
    T = cfg.T
    for core in range(ncore):
        b = core // 2; g = core % 2
        out[b, g * T:(g + 1) * T, :] = res.results[core]["outT"].T
    return out
```
